# Optimizing a Trainium2 kernel written in Bass

```python
import jax, jax.numpy as jnp
from jax import lax
import numpy as np

D_MODEL = 2048
BATCH = 8
SEQ = 4096
DEPTH = 4
DEC_BATCH = 16
DEC_SEQ = 64
PAST_LEN = 4096

CHUNK = 64
SUB = 16
NSUB = CHUNK // SUB
D_A = D_MODEL // 2
HGRN_DK = 128
HGRN_HEADS = D_A // HGRN_DK
HGRN_DV = D_A // HGRN_HEADS
D_B = D_MODEL // 2
HEAD_DIM = 64
N_Q = D_B // HEAD_DIM
N_KV = N_Q // 4
GROUP = N_Q // N_KV
WINDOW = 128
N_LOOKBACK = WINDOW // CHUNK
BAND = (N_LOOKBACK + 1) * CHUNK
ROT_DIM = HEAD_DIM // 4
ROPE_THETA = 500000.0
ATTN_SCALE = HEAD_DIM ** -0.5
EPS = 1e-6
NEG = -1e30
IN_SIZES = (D_A, D_A, D_A, D_A, D_A, N_Q * HEAD_DIM, N_KV * HEAD_DIM, N_KV * HEAD_DIM, D_B, D_MODEL, D_MODEL)
N_IN = sum(IN_SIZES)

kernel_name = 'hybrid_hgrn2_swa_sink_stream_step'


def _split_points():
    return [int(s) for s in np.cumsum(IN_SIZES)[:-1]]


def _rms_norm(x, g):
    xf = x.astype(jnp.float32)
    y = xf * lax.rsqrt(jnp.mean(xf * xf, axis=-1, keepdims=True) + EPS)
    return (y * g.astype(jnp.float32)).astype(x.dtype)


def _partial_rope(x, pos):
    inv = (ROPE_THETA ** (-np.arange(0, ROT_DIM, 2) / ROT_DIM)).astype(np.float32)
    ang = pos.astype(jnp.float32)[:, None] * inv[None, :]
    cos = jnp.cos(ang)[None, :, None, :]
    sin = jnp.sin(ang)[None, :, None, :]
    xr = x[..., :ROT_DIM].astype(jnp.float32)
    x1, x2 = xr[..., :ROT_DIM // 2], xr[..., ROT_DIM // 2:]
    rot = jnp.concatenate([x1 * cos - x2 * sin, x2 * cos + x1 * sin], axis=-1).astype(x.dtype)
    return jnp.concatenate([rot, x[..., ROT_DIM:]], axis=-1)


def _lower_bounds(lb_logits):
    p = jax.nn.softmax(lb_logits.astype(jnp.float32), axis=0)
    return jnp.cumsum(p, axis=0) - p[:1]


def _hgrn_chunk(s, q, g, k, v):
    bsz, nh, length, dk = q.shape
    b = jnp.cumsum(g, axis=2)
    o_inter = jnp.einsum('bhtd,bhde->bhte', q * jnp.exp(b), s)
    bs = b.reshape(bsz, nh, NSUB, SUB, dk)
    b_ref = jnp.concatenate([jnp.zeros_like(bs[:, :, :1, -1]), bs[:, :, :-1, -1]], axis=2)
    q_sub = q.reshape(bsz, nh, NSUB, SUB, dk)
    k_sub = k.reshape(bsz, nh, NSUB, SUB, dk)
    q_rel = q_sub * jnp.exp(bs - b_ref[:, :, :, None, :])
    earlier = (np.arange(length)[None, :] < np.arange(NSUB)[:, None] * SUB)[:, :, None]
    k_rel = k[:, :, None] * jnp.exp(jnp.where(earlier, b_ref[:, :, :, None, :] - b[:, :, None], NEG))
    a_off = jnp.einsum('bhjtd,bhjsd->bhjts', q_rel, k_rel)
    causal = (np.arange(SUB)[:, None] >= np.arange(SUB)[None, :])[:, :, None]
    decay = jnp.exp(jnp.where(causal, bs[:, :, :, :, None] - bs[:, :, :, None], NEG))
    a_diag = jnp.einsum('bhjtd,bhjsd,bhjtsd->bhjts', q_sub, k_sub, decay)
    a = a_off + jnp.einsum('bhjts,jm->bhjtms', a_diag,
                           jnp.eye(NSUB, dtype=a_diag.dtype)).reshape(bsz, nh, NSUB, SUB, length)
    o_intra = jnp.einsum('bhjts,bhse->bhjte', a, v).reshape(bsz, nh, length, -1)
    b_end = b[:, :, -1]
    s_new = jnp.exp(b_end)[..., None] * s + jnp.einsum('bhsd,bhse->bhde', k * jnp.exp(b_end[:, :, None] - b), v)
    return s_new, o_inter + o_intra


def _hgrn2(q, g, k, v, s0):
    bsz, nh, t, _ = q.shape
    pad = (-t) % CHUNK
    if pad:
        pw = ((0, 0), (0, 0), (0, pad), (0, 0))
        q, g, k, v = jnp.pad(q, pw), jnp.pad(g, pw), jnp.pad(k, pw), jnp.pad(v, pw)
    n = (t + pad) // CHUNK

    def to_chunks(a):
        return a.reshape(bsz, nh, n, CHUNK, a.shape[-1]).transpose(2, 0, 1, 3, 4)

    def step(s, xs):
        s_new, o = _hgrn_chunk(s, *xs)
        return s_new, o

    s_fin, o = lax.scan(step, s0, (to_chunks(q), to_chunks(g), to_chunks(k), to_chunks(v)))
    o = o.transpose(1, 2, 0, 3, 4).reshape(bsz, nh, n * CHUNK, -1)[:, :, :t]
    return o, s_fin


def _sink_attention(q, k, v, sinks, mask):
    s = jnp.einsum('...qkgd,...skd->...kgqs', q, k).astype(jnp.float32) * ATTN_SCALE
    s = jnp.where(mask, s, NEG)
    sink = sinks.astype(jnp.float32).reshape(N_KV, GROUP)[:, :, None, None]
    m = jnp.maximum(s.max(axis=-1, keepdims=True), sink)
    p = jnp.exp(s - m)
    w = p / (p.sum(axis=-1, keepdims=True) + jnp.exp(sink - m))
    return jnp.einsum('...kgqs,...skd->...qkgd', w.astype(v.dtype), v)


def _swa_prompt(q, k, v, sinks):
    bsz, t = q.shape[0], q.shape[1]
    n = t // CHUNK
    qc = q.reshape(bsz, n, CHUNK, N_KV, GROUP, HEAD_DIM)

    def band(a):
        ap = jnp.pad(a, ((0, 0), (WINDOW, 0), (0, 0), (0, 0))).reshape(bsz, n + N_LOOKBACK, CHUNK, N_KV, HEAD_DIM)
        return jnp.concatenate([ap[:, m:m + n] for m in range(N_LOOKBACK + 1)], axis=2)

    key_chunk = np.arange(n)[:, None] - N_LOOKBACK + np.arange(BAND)[None, :] // CHUNK
    mask = (key_chunk >= 0)[:, None, None, None, :]
    o = _sink_attention(qc, band(k), band(v), sinks, mask)
    return o.reshape(bsz, t, D_B)


def _layer(x, c, pos, s0, cache_k, cache_v, ada_w, ada_b, norm_g, w_in, lb, hgrn_g, sinks, w_pa, w_pb, w_out):
    bsz, t, _ = x.shape
    f32 = jnp.float32
    mod = jnp.dot(c, ada_w) + ada_b
    shift, scale, gate = jnp.split(mod, 3, axis=-1)
    h = _rms_norm(x, norm_g) * (1 + scale[:, None]) + shift[:, None]
    qa, fa, ia, ga, za, qb, kb, vb, zb, ma, mb = jnp.split(h @ w_in, _split_points(), axis=-1)

    a = fa.astype(f32)
    lbf = lb.astype(f32)
    log_f = jax.nn.log_sigmoid(a) + jnp.log1p(lbf * jnp.exp(-a))
    k_in = (1 - lbf) * jax.nn.sigmoid(-a)

    def heads(u):
        return u.astype(f32).reshape(bsz, t, HGRN_HEADS, -1).transpose(0, 2, 1, 3)

    o_a, s_new = _hgrn2(heads(jax.nn.silu(qa.astype(f32))), heads(log_f), heads(k_in), heads(ia), s0.astype(f32))
    o_a = _rms_norm(o_a.transpose(0, 2, 1, 3), hgrn_g.reshape(HGRN_HEADS, HGRN_DV)).reshape(bsz, t, D_A)
    u_a = (o_a * jax.nn.sigmoid(ga.astype(f32)) * jax.nn.silu(za.astype(f32))).astype(x.dtype)

    q = _partial_rope(qb.reshape(bsz, t, N_Q, HEAD_DIM), pos)
    k = _partial_rope(kb.reshape(bsz, t, N_KV, HEAD_DIM), pos)
    v = vb.reshape(bsz, t, N_KV, HEAD_DIM)
    if cache_k is None:
        o_b = _swa_prompt(q, k, v, sinks)
        k_new, v_new = k[:, -WINDOW:], v[:, -WINDOW:]
    else:
        kk = jnp.concatenate([cache_k.astype(k.dtype), k], axis=1)
        vv = jnp.concatenate([cache_v.astype(v.dtype), v], axis=1)
        o_b = _sink_attention(q.reshape(bsz, t, N_KV, GROUP, HEAD_DIM), kk, vv, sinks, True).reshape(bsz, t, D_B)
        k_new, v_new = k, v
    u_b = o_b * jax.nn.silu(zb)

    merged = jax.nn.sigmoid(ma) * (u_a @ w_pa) + jax.nn.sigmoid(mb) * (u_b @ w_pb)
    y = x + gate[:, None] * (merged @ w_out)
    return y, s_new, k_new, v_new


def setup_inputs(seed: int = 0) -> dict:
    key = jax.random.key(seed)
    ks = jax.random.split(key, 20)
    nrm = jax.random.normal
    d = D_MODEL
    return {
        'x_prompt': nrm(ks[0], (BATCH, SEQ, d), jnp.float32),
        'x_sample': nrm(ks[1], (DEC_BATCH, DEC_SEQ, d), jnp.float32),
        'c_prompt': nrm(ks[2], (BATCH, d), jnp.float32),
        'c_sample': nrm(ks[3], (DEC_BATCH, d), jnp.float32),
        'state_hgrn': 0.5 * nrm(ks[4], (DEPTH, DEC_BATCH, HGRN_HEADS, HGRN_DK, HGRN_DV), jnp.float32),
        'cache_win_k': nrm(ks[5], (DEPTH, DEC_BATCH, WINDOW, N_KV, HEAD_DIM), jnp.float32),
        'cache_win_v': nrm(ks[6], (DEPTH, DEC_BATCH, WINDOW, N_KV, HEAD_DIM), jnp.float32),
        'ada_w': 0.5 * d ** -0.5 * nrm(ks[7], (DEPTH, d, 3 * d), jnp.float32),
        'ada_b': 0.02 * nrm(ks[8], (DEPTH, 3 * d), jnp.float32),
        'norm_g': 1.0 + 0.02 * nrm(ks[9], (DEPTH, d), jnp.float32),
        'w_in': d ** -0.5 * nrm(ks[10], (DEPTH, d, N_IN), jnp.float32),
        'lb_logits': 0.5 * nrm(ks[11], (DEPTH, D_A), jnp.float32),
        'hgrn_norm_g': 1.0 + 0.02 * nrm(ks[12], (DEPTH, D_A), jnp.float32),
        'sinks': 0.5 * nrm(ks[13], (DEPTH, N_Q), jnp.float32),
        'w_branch_a': D_A ** -0.5 * nrm(ks[14], (DEPTH, D_A, d), jnp.float32),
        'w_branch_b': D_B ** -0.5 * nrm(ks[15], (DEPTH, D_B, d), jnp.float32),
        'w_out': d ** -0.5 * nrm(ks[16], (DEPTH, d, d), jnp.float32),
        'final_norm_g': 1.0 + 0.02 * nrm(ks[17], (d,), jnp.float32),
    }


def reference(x_prompt, x_sample, c_prompt, c_sample, state_hgrn, cache_win_k, cache_win_v,
              ada_w, ada_b, norm_g, w_in, lb_logits, hgrn_norm_g, sinks, w_branch_a, w_branch_b, w_out,
              final_norm_g):
    lb = _lower_bounds(lb_logits)
    bp, tp = x_prompt.shape[0], x_prompt.shape[1]
    ts = x_sample.shape[1]
    pos_p = jnp.arange(tp)
    pos_s = PAST_LEN + jnp.arange(ts)
    hp, hs = x_prompt, x_sample
    sp_l, kp_l, vp_l, ss_l, ks_l, vs_l = [], [], [], [], [], []
    for l in range(DEPTH):
        w = (ada_w[l], ada_b[l], norm_g[l], w_in[l], lb[l], hgrn_norm_g[l], sinks[l],
             w_branch_a[l], w_branch_b[l], w_out[l])
        s0 = jnp.zeros((bp, HGRN_HEADS, HGRN_DK, HGRN_DV), jnp.float32)
        hp, s_p, k_p, v_p = _layer(hp, c_prompt, pos_p, s0, None, None, *w)
        hs, s_s, k_s, v_s = _layer(hs, c_sample, pos_s, state_hgrn[l], cache_win_k[l], cache_win_v[l], *w)
        sp_l.append(s_p)
        kp_l.append(k_p)
        vp_l.append(v_p)
        ss_l.append(s_s)
        ks_l.append(k_s)
        vs_l.append(v_s)
    y_prompt = _rms_norm(hp, final_norm_g)
    y_sample = _rms_norm(hs, final_norm_g)
    return (y_prompt, y_sample, jnp.stack(sp_l), jnp.stack(kp_l), jnp.stack(vp_l),
            jnp.stack(ss_l), jnp.stack(ks_l), jnp.stack(vs_l))
```

```python
import contextlib
import numpy as np
import ml_dtypes
import concourse.bass as bass
import concourse.mybir as mybir
from concourse.bass_utils import run_bass_kernel_spmd

F32 = mybir.dt.float32
BF16 = mybir.dt.bfloat16
AF = mybir.ActivationFunctionType
ALU = mybir.AluOpType
AX = mybir.AxisListType

D = 2048
KC = 16
NIN = 11776
OFF = dict(qa=0, fa=1024, ia=2048, ga=3072, za=4096, qb=5120, kb=6144, vb=6400, zb=6656, ma=7680, mb=9728)
PAST = 4096
EPS = 1e-6
CFG = dict(SEQ=4096, DEPTH=4, NCORES=8, NBUF=2)
NGRP = 35
GW = 8192
import os
STOP = os.environ.get("KSTOP", "")
SKIP = os.environ.get("KSKIP", "")
LAST = {}


class _Stop(Exception):
    pass


class _Instr:
    __slots__ = ("chan", "idx", "needs_inc", "value", "fn", "waits", "eng", "is_dma")

    def __init__(self, chan, idx, fn, eng, is_dma):
        self.chan, self.idx, self.fn, self.eng, self.is_dma = chan, idx, fn, eng, is_dma
        self.needs_inc = is_dma
        self.value = None
        self.waits = []


class Sched:
    ENGS = ("pe", "act", "dve", "pool", "sp")

    def __init__(self):
        self.lists = {e: [] for e in self.ENGS}
        self.chan_n, self.chan_last = {}, {}
        self.seen = {e: {} for e in self.ENGS}
        self.lw, self.rd = {}, {}
        self.dma_chans = []

    def _dep(self, eng, ins, p):
        if p is None or p is ins:
            return
        if p.chan == eng and eng == "pe":
            return
        s = self.seen[eng]
        if s.get(p.chan, -1) >= p.idx:
            return
        s[p.chan] = p.idx
        p.needs_inc = True
        ins.waits.append(p)

    PSUM_BANK = {"pj0": "pj0", "pj1": "pj1", "pj2": "pj2", "trp": "trp", "atp": "atp", "otp": "otp", "otp0": "otp",
                 "otp1": "otp", "dsp": "dsp", "dsp0": "dsp", "dsp1": "dsp", "dsp2": "dsp", "dsp3": "dsp", "dnp": "dnp"}

    def op(self, eng, fn, reads=(), writes=(), dma=None):
        pb = self.PSUM_BANK
        banks = [pb[k] for k in list(reads) + list(writes) if k in pb]
        reads = [k for k in reads if k not in pb]
        writes = [k for k in writes if k not in pb] + sorted(set(banks))
        chan = dma if dma is not None else eng
        if dma is not None and chan not in self.chan_n:
            self.dma_chans.append(chan)
        idx = self.chan_n.get(chan, 0)
        self.chan_n[chan] = idx + 1
        ins = _Instr(chan, idx, fn, eng, dma is not None)
        if dma is not None:
            self._dep(eng, ins, self.chan_last.get(chan))
        self.chan_last[chan] = ins
        for k in reads:
            self._dep(eng, ins, self.lw.get(k))
        for k in writes:
            self._dep(eng, ins, self.lw.get(k))
            for r in self.rd.get(k, ()):
                self._dep(eng, ins, r)
        for k in writes:
            self.lw[k] = ins
            self.rd[k] = []
        for k in reads:
            if k not in writes:
                self.rd.setdefault(k, []).append(ins)
        self.lists[eng].append(ins)
        return ins

    def wait_all(self, eng, instrs):
        ins = _Instr(eng, -1, None, eng, False)
        for p in instrs:
            self._dep(eng, ins, p)
        self.lists[eng].append(ins)

    def finalize(self):
        for eng in self.ENGS:
            c = 0
            for ins in self.lists[eng]:
                if ins.fn is None:
                    continue
                if ins.is_dma:
                    ins.value = 16 * (ins.idx + 1)
                elif ins.needs_inc:
                    c += 1
                    ins.value = c

    def replay(self, eng, handle, sems):
        for ins in self.lists[eng]:
            for p in ins.waits:
                handle.wait_ge(sems[p.chan], p.value)
            if ins.fn is None:
                continue
            r = ins.fn(handle)
            if ins.is_dma:
                r.then_inc(sems[ins.chan], 16)
            elif ins.needs_inc:
                r.then_inc(sems[ins.chan], 1)


def build_nc(SEQ, DEPTH, NBUF):
    NPT = SEQ // 512
    NTOK = SEQ + 128
    NTB = SEQ // 128 + 1
    nc = bass.Bass("TRN2", target_bir_lowering=False)
    S = Sched()

    def din(name, shape, dt=F32):
        return nc.dram_tensor(name, list(shape), dt, kind="ExternalInput").ap()

    def dout(name, shape):
        return nc.dram_tensor(name, list(shape), F32, kind="ExternalOutput").ap()

    xT_d = din("xT", [KC, 128, NTOK])
    cT_d = din("cT", [128, KC, 3])
    adaw_d = din("adaw", [DEPTH * 12, 128, GW])
    adab_d = din("adab", [128, DEPTH, 48])
    normg_d = din("normg", [128, DEPTH, KC])
    lbl_d = din("lbl", [128, 8, 4])
    hg_d = din("hg", [128, DEPTH, 8])
    sinks_d = din("sinks", [DEPTH * 16])
    fng_d = din("fng", [128, KC])
    rope_d = din("rope", [128, NTB, 32])
    ident_d = din("ident", [128, 128], BF16)
    mask32_d = din("mask32", [32, 32], BF16)
    scanm_d = din("scanm", [128, 512], BF16)
    wst_d = din("wst", [DEPTH * NGRP, 128, GW])
    sth_d = din("sth", [DEPTH, 2, 8, 128, 128])
    cwk_d = din("cwk", [DEPTH, 2, 128, 256])
    cwv_d = din("cwv", [DEPTH, 2, 128, 256])

    yT_d = dout("yT", [KC, 128, NTOK])
    stp_d = dout("stp", [DEPTH, 8, 128, 128])
    wkp_d = dout("wkp", [DEPTH, 128, 256])
    wvp_d = dout("wvp", [DEPTH, 128, 256])
    sts_d = dout("sts", [DEPTH, 2, 8, 128, 128])
    nks_d = dout("nks", [DEPTH, 2, 64, 256])
    nvs_d = dout("nvs", [DEPTH, 2, 64, 256])

    es = contextlib.ExitStack()
    with es:
        def sb(name, shape, dt=F32):
            return es.enter_context(nc.sbuf_tensor(name, list(shape), dt))

        def ps(name, shape, dt=F32):
            return es.enter_context(nc.psum_tensor(name, list(shape), dt))

        wbuf = [sb(f"wbuf{i}", [128, GW], BF16) for i in range(NBUF)]
        xT = sb("xTs", [128, KC, 512])
        hT = sb("hT", [128, KC, 512], BF16)
        big16 = sb("big16", [128, KC * 512], BF16)
        qT = big16[0:64, :].rearrange("p (h t) -> p h t", t=512)
        mg16 = big16[:, :].rearrange("p (k t) -> p k t", t=512)
        NSCR = 3
        scr = [sb(f"scr{i}", [128, 512]) for i in range(NSCR)]
        qtok = [sb(f"qtok{i}", [128, 512], BF16) for i in range(2)]
        kv32 = sb("kv32", [128, 512])
        k16 = [sb(f"k16_{i}", [128, 256], BF16) for i in range(2)]
        vdup = sb("vdup", [128, 5, 4, 128], BF16)
        kT = sb("kT", [64, 4, 640], BF16)
        khist = sb("khist", [64, DEPTH, 4, 128], BF16)
        vhist = sb("vhist", [128, DEPTH, 4, 128], BF16)
        ua16 = sb("ua16", [128, 8, 512], BF16)
        ub16 = sb("ub16", [128, 8, 512], BF16)
        gatea = [[sb(f"gatea{p}{i}", [128, 512], BF16) for i in range(2)] for p in range(2)]
        Qt16 = [[sb(f"Qt16_{p}{i}", [128, 512], BF16) for i in range(2)] for p in range(2)]
        Kt16 = [sb(f"Kt16_{i}", [128, 512], BF16) for i in range(2)]
        kh16 = [sb(f"kh16_{i}", [128, 512], BF16) for i in range(2)]
        Va16 = [sb(f"Va16_{i}", [128, 512], BF16) for i in range(2)]
        khtok = [sb(f"khtok{i}", [32, 16, 128], BF16) for i in range(2)]
        vtok = [sb(f"vtok{i}", [32, 16, 128], BF16) for i in range(2)]
        AT16 = [sb(f"AT16_{i}", [32, 512], BF16) for i in range(2)]
        eend = [[sb(f"eend{p}{i}", [128, 16]) for i in range(2)] for p in range(2)]
        o32 = [sb(f"o32_{i}", [128, 512]) for i in range(2)]
        S32x = [[sb(f"S32_{i}_{p}", [128, 128]) for p in range(2)] for i in range(2)]
        S16q = [sb(f"S16q_{i}", [128, 4, 128], BF16) for i in range(2)]
        PT = [sb(f"PT{i}", [128, 512], BF16) for i in range(2)]
        sq16 = [sb(f"sq16_{i}", [128, 512], BF16) for i in range(2)]
        sgt16 = [sb(f"sgt16_{i}", [128, 512], BF16) for i in range(2)]
        faS = [[sb(f"faS{i}{k}", [128, 512]) for k in range(3)] for i in range(2)]
        rstdS = sb("rstdS", [128, 512])
        ident16 = sb("ident16", [128, 128], BF16)
        ones16 = sb("ones16", [128, 128], BF16)
        mask32 = sb("mask32s", [32, 32], BF16)
        scanm = sb("scanms", [128, 512], BF16)
        rope = sb("ropes", [128, 4, 32])
        cT16 = sb("cT16", [128, KC, 3], BF16)
        modT = sb("modT", [128, DEPTH, 48, 3])
        Gm = sb("Gm", [128, DEPTH, KC, 3])
        fng = sb("fngs", [128, KC])
        lbv = sb("lbv", [128, 8, 4])
        lb1 = sb("lb1", [128, 8, 4])
        nlb1 = sb("nlb1", [128, 8, 4])
        hg = sb("hgs", [128, DEPTH, 8])
        esink = sb("esink", [128, DEPTH * 16])

        _o = o32[0]
        adab = _o[:, 0:DEPTH * 48].rearrange("p (l c) -> p l c", c=48)
        normg = _o[:, 192:192 + DEPTH * KC].rearrange("p (l c) -> p l c", c=KC)
        lbl = _o[:, 256:288].rearrange("p (h l) -> p h l", l=4)
        lbe = _o[:, 288:320].rearrange("p (h l) -> p h l", l=4)
        lbs = _o[:, 320:328]
        cT32 = _o[:, 328:376].rearrange("p (k s) -> p k s", s=3)
        PJ = [ps(f"pj{i}", [128, 512]) for i in range(3)]
        trp = ps("trp", [128, 1024], BF16)
        atp = ps("atp", [128, 512])
        otp = ps("otp", [128, 512])
        dsp = ps("dsp", [128, 512])
        dnp = ps("dnp", [128, 512])

        sem_names = list(S.ENGS) + ["ld", "st", "st2", "io"] + [f"w{i}" for i in range(NBUF)]
        sems = {n: es.enter_context(nc.semaphore(n)) for n in sem_names}

        cnt = {"pj": 0, "scr": 0, "st": 0}

        def next_pj():
            i = cnt["pj"] % 3
            cnt["pj"] += 1
            return PJ[i], f"pj{i}"

        def next_scr():
            i = cnt["scr"] % NSCR
            cnt["scr"] += 1
            return scr[i], f"scr{i}"

        def A(eng, out, in_, func, reads, writes, **kw):
            S.op(eng, lambda e: e.activation(out=out, in_=in_, func=func, **kw), reads, writes)

        def ACT(out, in_, func, reads, writes, **kw):
            A("act", out, in_, func, reads, writes, **kw)

        def TT(eng, out, in0, in1, op, reads, writes):
            S.op(eng, lambda e: e.tensor_tensor(out=out, in0=in0, in1=in1, op=op), reads, writes)

        def TS(eng, out, in0, s1, s2, op0, op1, reads, writes):
            S.op(eng, lambda e: e.tensor_scalar(out=out, in0=in0, scalar1=s1, scalar2=s2, op0=op0, op1=op1), reads, writes)

        def STT(out, in0, scalar, in1, op0, op1, reads, writes):
            S.op("dve", lambda e: e.scalar_tensor_tensor(out=out, in0=in0, scalar=scalar, in1=in1, op0=op0, op1=op1), reads, writes)

        def CP(eng, out, in_, reads, writes):
            if eng == "act":
                ACT(out, in_, AF.Copy, reads, writes)
            else:
                S.op(eng, lambda e: e.tensor_copy(out=out, in_=in_), reads, writes)

        def MS(eng, ap, val, writes):
            S.op(eng, lambda e: e.memset(ap, val), (), writes)

        def LD(out, in_, writes, chan="ld"):
            return S.op("sp", lambda e: e.dma_start(out=out, in_=in_), (), writes, dma=chan)

        out_dmas = []

        def ST(out, in_, reads, extra_writes=()):
            chan = ("st", "st2")[cnt["st"] % 2]
            cnt["st"] += 1
            ins = S.op("sp", lambda e: e.dma_start(out=out, in_=in_), reads, extra_writes, dma=chan)
            out_dmas.append(ins)
            return ins

        def DBG(name, ap, shape, reads):
            d = nc.dram_tensor("dbg_" + name, list(shape), ap.dtype, kind="ExternalOutput").ap()
            ST(d, ap, reads)

        def stop(phase):
            if STOP == phase:
                raise _Stop()

        class WS:
            def __init__(self):
                self.plan = []
                self.nissued = 0
                self.nget = 0

            def add(self, ap, ncols=GW):
                self.plan.append((ap, ncols))

            def _issue(self):
                if self.nissued >= len(self.plan):
                    return
                i = self.nissued
                b = i % NBUF
                ap, ncols = self.plan[i]
                S.op("pool", lambda e: e.dma_start(out=wbuf[b][:, 0:ncols], in_=ap[:, 0:ncols]),
                     (), [f"wbuf{b}"], dma=f"w{b}")
                self.nissued += 1

            def start(self):
                for _ in range(NBUF):
                    self._issue()

            def get(self):
                b = self.nget % NBUF
                assert self.nget < self.nissued
                self.nget += 1
                return wbuf[b], f"wbuf{b}"

            def done(self):
                self._issue()

        ws = WS()
        for g in range(DEPTH * 12):
            ws.add(adaw_d[g])
        tiles = [(t, 512) for t in range(NPT)] + [(NPT, 128)]
        for (ti, T) in tiles:
            for l in range(DEPTH):
                for g in range(NGRP):
                    ncols = 6144 if 15 <= g < 31 else GW
                    ws.add(wst_d[l * NGRP + g], ncols)

        LD(ident16[:], ident_d, ["ident16"])
        LD(mask32[:], mask32_d, ["mask32"])
        LD(scanm[:], scanm_d, ["scanm"])
        LD(cT32[:], cT_d, ["cT32"])
        LD(adab[:], adab_d, ["adab"])
        LD(normg[:], normg_d, ["normg"])
        LD(fng[:], fng_d, ["fng"])
        LD(lbl[:], lbl_d, ["lbl"])
        LD(hg[:], hg_d, ["hg"])
        LD(esink[:], sinks_d.partition_broadcast(128), ["esink"])
        ws.start()
        MS("dve", ones16[:], 1.0, ["ones16"])
        CP("dve", cT16[:], cT32[:], ["cT32"], ["cT16"])
        ACT(esink[:], esink[:], AF.Exp, ["esink"], ["esink"])
        ACT(lbe[:], lbl[:], AF.Exp, ["lbl"], ["lbe"])
        S.op("dve", lambda e: e.tensor_reduce(out=lbs[:], in_=lbe[:], axis=AX.X, op=ALU.add), ["lbe"], ["lbs"])
        S.op("dve", lambda e: e.reciprocal(out=lbs[:], in_=lbs[:]), ["lbs"], ["lbs"])
        TT("dve", lbe[:], lbe[:], lbs[:].unsqueeze(2).to_broadcast([128, 8, 4]), ALU.mult, ["lbe", "lbs"], ["lbe"])
        MS("dve", lbv[:], 0.0, ["lbv"])
        for l in range(1, 4):
            TT("dve", lbv[:, :, l:l + 1], lbv[:, :, l - 1:l], lbe[:, :, l:l + 1], ALU.add, ["lbv", "lbe"], ["lbv"])
        TS("dve", lb1[:], lbv[:], -1.0, 1.0, ALU.mult, ALU.add, ["lbv"], ["lb1"])
        TS("dve", nlb1[:], lbv[:], 1.0, -1.0, ALU.mult, ALU.add, ["lbv"], ["nlb1"])

        for l in range(DEPTH):
            for g in range(12):
                w, wk = ws.get()
                wv = w[:, :].rearrange("p (k n) -> p k n", n=512)
                pj, pk = next_pj()

                def mm(e, wv=wv, pj=pj):
                    for i in range(4):
                        for kc in range(KC):
                            ins = e.matmul(pj[:, i * 4:i * 4 + 3], lhsT=wv[:, kc, i * 128:(i + 1) * 128],
                                           rhs=cT16[:, kc, :], start=(kc == 0), stop=(kc == KC - 1))
                    return ins
                S.op("pe", mm, [wk, "cT16"], [pk])
                ws.done()
                TT("dve", modT[:, l, g * 4:(g + 1) * 4, :], pj[:, 0:16].rearrange("p (i s) -> p i s", s=4)[:, :, 0:3],
                   adab[:, l, g * 4:(g + 1) * 4].unsqueeze(2).to_broadcast([128, 4, 3]), ALU.add,
                   [pk, "adab"], ["modT"])
        for l in range(DEPTH):
            TS("dve", Gm[:, l, :, :], modT[:, l, 16:32, :], 1.0, None, ALU.add, ALU.bypass, ["modT"], ["Gm"])
            TT("dve", Gm[:, l, :, :], Gm[:, l, :, :], normg[:, l, :].unsqueeze(2).to_broadcast([128, KC, 3]),
               ALU.mult, ["Gm", "normg"], ["Gm"])

        if STOP == "mod":
            DBG("modT", modT[:].rearrange("p l c s -> p (l c s)"), [128, DEPTH * 48 * 3], ["modT"])
            DBG("Gm", Gm[:].rearrange("p l c s -> p (l c s)"), [128, DEPTH * KC * 3], ["Gm"])
            DBG("lbv", lbv[:].rearrange("p h l -> p (h l)"), [128, 32], ["lbv"])
            DBG("esink", esink[:], [128, DEPTH * 16], ["esink"])
        def rms_rstd(src_fn, nchunk, T, key_reads, scale, pre=False):
            for kc in range(0 if pre else nchunk):
                sq, sk = sq16[kc % 2], f"sq16_{kc % 2}"
                ACT(sq[:, 0:T], src_fn(kc), AF.Square, key_reads, [sk])
                S.op("pe", lambda e, sq=sq, kc=kc: e.matmul(dnp[:, 0:T], lhsT=ones16[:], rhs=sq[:, 0:T],
                                                           start=(kc == 0), stop=(kc == nchunk - 1)),
                     [sk, "ones16"], ["dnp"])
            ln, lk = rstdS, "rstdS"
            ACT(ln[:, 0:T], dnp[:, 0:T], AF.Ln, ["dnp"], [lk], scale=scale, bias=EPS)
            ACT(ln[:, 0:T], ln[:, 0:T], AF.Exp, [lk], [lk], scale=-0.5)
            return ln, lk

        def segs_of(T):
            return [(0, 512, 0)] if T == 512 else [(0, 64, 1), (64, 64, 2)]

        def layer_tile(ti, T, l, is_first_tile, is_last_prompt_tile):
            sample = (T == 128)
            segs = segs_of(T)
            blocks = [(i * 128, 128, 0) for i in range(4)] if not sample else [(0, 64, 1), (64, 64, 2)]
            nch = T // 32
            rstd, rk = rms_rstd(lambda kc: xT[:, kc, 0:T], KC, T, ["xT"], 1.0 / D, pre=(l > 0))
            for kc in range(KC):
                tmp, tk = next_scr()
                for (c0, n, s) in segs:
                    STT(tmp[:, c0:c0 + n], xT[:, kc, c0:c0 + n], Gm[:, l, kc, s:s + 1], rstd[:, c0:c0 + n],
                        ALU.mult, ALU.mult, ["xT", "Gm", rk], [tk])
                    ACT(hT[:, kc, c0:c0 + n], tmp[:, c0:c0 + n], AF.Identity, [tk, "modT"], ["hT"],
                        bias=modT[:, l, kc, s:s + 1])

            if STOP == "norm":
                DBG("hT", hT[:].rearrange("p k t -> p (k t)"), [128, KC * 512], ["hT"])
            stop("norm")
            def rope_fix(pj, pk, n, nh, slot, out_view, out_key):
                if "r" in SKIP:
                    return
                pv = pj[0:n, 0:nh * 64].rearrange("p (h d) -> p h d", d=64)
                t1, k1 = next_scr()
                t2, k2 = next_scr()
                t1v = t1[0:n, 0:nh * 16].rearrange("p (h d) -> p h d", d=16)
                t2v = t2[0:n, 0:nh * 16].rearrange("p (h d) -> p h d", d=16)
                TT("dve", t1v, pv[:, :, 0:16], rope[0:n, slot, 0:16].unsqueeze(1).to_broadcast([n, nh, 16]),
                   ALU.mult, [pk, "rope"], [k1])
                TT("dve", t2v[:, :, 0:8], pv[:, :, 8:16], rope[0:n, slot, 16:24].unsqueeze(1).to_broadcast([n, nh, 8]),
                   ALU.mult, [pk, "rope"], [k2])
                TT("dve", t2v[:, :, 8:16], pv[:, :, 0:8], rope[0:n, slot, 24:32].unsqueeze(1).to_broadcast([n, nh, 8]),
                   ALU.mult, [pk, "rope"], [k2])
                TT("dve", out_view, t1v, t2v, ALU.add, [k1, k2], [out_key])

            if not sample and not is_first_tile:
                CP("dve", kT[:, :, 0:128], khist[:, l, :, :], ["khist"], ["kT"])
                CP("dve", vdup[:, 0, :, :], vhist[:, l, :, :], ["vhist"], ["vdup0"])
            pend_tr = []
            for gi in range(3):
                w, wk = ws.get()
                wv = w[:, :].rearrange("p (k n) -> p k n", n=512)
                for bi, (c0, n, s) in enumerate(blocks):
                    slot = bi if not sample else 0
                    pj, pk = next_pj()

                    def mm(e, wv=wv, pj=pj, c0=c0, n=n):
                        for kc in range(KC):
                            ins = e.matmul(pj[0:n, :], lhsT=hT[:, kc, c0:c0 + n], rhs=wv[:, kc, :],
                                           start=(kc == 0), stop=(kc == KC - 1))
                        return ins
                    S.op("pe", mm, [wk, "hT"], [pk])
                    while pend_tr:
                        pend_tr.pop(0)()
                    rb = bi % 2
                    if gi < 2:
                        qt_, qk_ = qtok[rb], f"qtok{rb}"
                        ACT(qt_[0:n, 0:512], pj[0:n, :], AF.Copy, [pk], [qk_])
                        rope_fix(pj, pk, n, 8, slot,
                                 qt_[0:n, 0:512].rearrange("p (h d) -> p h d", d=64)[:, :, 0:16], qk_)

                        def do_trq(qt_=qt_, qk_=qk_, n=n, c0=c0, gi=gi, bi=bi):
                            def trq(e):
                                for hh in range(8):
                                    ins = e.transpose(trp[0:64, hh * 128:hh * 128 + n], qt_[0:n, hh * 64:(hh + 1) * 64],
                                                      ident16[0:n, 0:n])
                                return ins
                            S.op("pe", trq, [qk_, "ident16"], ["trp"])
                            CP("dve" if bi % 2 == 0 else "act", qT[:, gi * 8:(gi + 1) * 8, c0:c0 + n],
                               trp[0:64, :].rearrange("p (h t) -> p h t", t=128)[:, :, 0:n], ["trp", "big16"], ["qT"])
                        pend_tr.append(do_trq)
                    else:
                        k16_, k16k = k16[rb], f"k16_{rb}"
                        ACT(kv32[0:n, :], pj[0:n, :], AF.Copy, [pk], ["kv32"])
                        rope_fix(pj, pk, n, 4, slot,
                                 kv32[0:n, 0:256].rearrange("p (h d) -> p h d", d=64)[:, :, 0:16], "kv32")
                        CP("dve", k16_[0:n, :], kv32[0:n, 0:256], ["kv32"], [k16k])
                        CP("dve", vdup[0:n, 1 + bi, :, :].rearrange("p g (r d) -> p g r d", r=2),
                           kv32[0:n, 256:512].rearrange("p (g d) -> p g d", d=64).unsqueeze(2).to_broadcast([n, 4, 2, 64]),
                           ["kv32"], [f"vdup{1 + bi}"])
                        if sample:
                            ST(nks_d[l, s - 1], kv32[0:n, 0:256], ["kv32"])
                            ST(nvs_d[l, s - 1], kv32[0:n, 256:512], ["kv32"])
                        elif is_last_prompt_tile and bi == 3:
                            ST(wkp_d[l], kv32[0:n, 0:256], ["kv32"])
                            ST(wvp_d[l], kv32[0:n, 256:512], ["kv32"])

                        def do_trk(k16_=k16_, k16k=k16k, n=n, c0=c0):
                            def trk(e):
                                for g in range(4):
                                    ins = e.transpose(trp[0:64, g * 128:g * 128 + n], k16_[0:n, g * 64:(g + 1) * 64], ident16[0:n, 0:n])
                                return ins
                            S.op("pe", trk, [k16k, "ident16"], ["trp"])
                            CP("dve", kT[:, :, 128 + c0:128 + c0 + n], trp[0:64, 0:512].rearrange("p (g t) -> p g t", t=128)[:, :, 0:n],
                               ["trp"], ["kT"])
                        pend_tr.append(do_trk)
                ws.done()
            while pend_tr:
                pend_tr.pop(0)()

            if STOP == "tm":
                DBG("qT", big16[0:64, :], [64, KC * 512], ["qT"])
                DBG("kT", kT[:, :, 128:640], [64, 4, 512], ["kT"])
                DBG("vdup", vdup[:, 1:5].rearrange("p b g d -> p (b g d)"), [128, 4 * 4 * 128], [f"vdup{i}" for i in range(5)])
            stop("tm")
            def fm_chunk(wv, wk, i):
                pj, pk = next_pj()

                def mm(e, wv=wv, pj=pj, i=i):
                    for kc in range(KC):
                        ins = e.matmul(pj[:, 0:T], lhsT=wv[:, kc, i * 128:(i + 1) * 128], rhs=hT[:, kc, 0:T],
                                       start=(kc == 0), stop=(kc == KC - 1))
                    return ins
                S.op("pe", mm, [wk, "hT"], [pk])
                return pj, pk

            def fm_pair(hb, pp):
                tails = []
                w, wk = ws.get()
                wv = w[:, :].rearrange("p (k n) -> p k n", n=512)
                for hi in range(2):
                    h = 2 * hb + hi
                    pj, pk = fm_chunk(wv, wk, hi)
                    E, ek = faS[hi][0], f"faE{hi}"
                    L1, l1k = faS[hi][1], f"faL{hi}"
                    L2, l2k = faS[hi][2], f"faM{hi}"
                    ACT(E[:, 0:T], pj[:, 0:T], AF.Exp, [pk], [ek], scale=-1.0)
                    ACT(L1[:, 0:T], E[:, 0:T], AF.Ln, [ek], [l1k], bias=1.0)
                    ACT(E[:, 0:T], L1[:, 0:T], AF.Exp, [l1k], [ek], scale=-1.0)
                    ACT(L2[:, 0:T], E[:, 0:T], AF.Ln, [ek, "lbv", "lb1"], [l2k], scale=lb1[:, h, l:l + 1],
                        bias=lbv[:, h, l:l + 1])
                    S.op("dve", lambda e, L1=L1, L2=L2: e.tensor_tensor_scan(out=L1[:, 0:T], data0=scanm[:, 0:T], data1=L2[:, 0:T],
                                                                          initial=0.0, op0=ALU.mult, op1=ALU.add),
                         ["scanm", l2k], [l1k])
                    TS("dve", E[:, 0:T], E[:, 0:T], nlb1[:, h, l:l + 1], lb1[:, h, l:l + 1], ALU.mult, ALU.add,
                       [ek, "nlb1", "lb1"], [ek])

                    def tail(hi=hi, E=E, ek=ek, L1=L1, l1k=l1k, L2=L2, l2k=l2k):
                        enb, enk = next_scr()
                        ACT(L2[:, 0:T], L1[:, 0:T], AF.Exp, [l1k], [l2k])
                        ACT(enb[:, 0:T], L1[:, 0:T], AF.Exp, [l1k], [enk], scale=-1.0)
                        TT("dve", Kt16[hi][:, 0:T], E[:, 0:T], enb[:, 0:T], ALU.mult, [ek, enk], [f"Kt16_{hi}"])
                        ebv = L2[:, 0:T].rearrange("p (c s) -> p c s", s=32)
                        TT("dve", kh16[hi][:, 0:T].rearrange("p (c s) -> p c s", s=32),
                           Kt16[hi][:, 0:T].rearrange("p (c s) -> p c s", s=32),
                           ebv[:, :, 31:32].to_broadcast([128, nch, 32]), ALU.mult, [f"Kt16_{hi}", l2k], [f"kh16_{hi}"])
                        CP("dve", eend[pp][hi][:, 0:nch], ebv[:, :, 31], [l2k], [f"eend{pp}{hi}"])
                    tails.append(tail)
                    if hi == 1:
                        tails.pop(0)()
                    yield hi
                for hi in range(2):
                    pj, pk = fm_chunk(wv, wk, 2 + hi)
                    ACT(Va16[hi][:, 0:T], pj[:, 0:T], AF.Copy, [pk], [f"Va16_{hi}"])
                    if hi == 0:
                        tails.pop(0)()
                    yield 2 + hi
                ws.done()
                w, wk = ws.get()
                wv = w[:, :].rearrange("p (k n) -> p k n", n=512)
                for hi in range(2):
                    pj, pk = fm_chunk(wv, wk, hi)
                    q, qk = next_scr()
                    ACT(q[:, 0:T], pj[:, 0:T], AF.Silu, [pk], [qk])
                    TT("dve", Qt16[pp][hi][:, 0:T], q[:, 0:T], faS[hi][2][:, 0:T], ALU.mult, [qk, f"faM{hi}"], [f"Qt16_{pp}{hi}"])
                    yield 4 + hi
                for hi in range(2):
                    pj, pk = fm_chunk(wv, wk, 2 + hi)
                    ACT(sgt16[hi][:, 0:T], pj[:, 0:T], AF.Silu, [pk], [f"sgt16_{hi}"])
                    yield 6 + hi
                ws.done()
                w, wk = ws.get()
                wv = w[:, :].rearrange("p (k n) -> p k n", n=512)
                for jj in range(2):
                    j = 2 * hb + jj
                    pj, pk = fm_chunk(wv, wk, jj)
                    ACT(ub16[:, j, 0:T], pj[:, 0:T], AF.Silu, [pk], [f"ub{j}"])
                    yield 8 + jj
                for hi in range(2):
                    pj, pk = fm_chunk(wv, wk, 2 + hi)
                    sg_, sgk = next_scr()
                    ACT(sg_[:, 0:T], pj[:, 0:T], AF.Sigmoid, [pk], [sgk])
                    TT("dve", gatea[pp][hi][:, 0:T], sg_[:, 0:T], sgt16[hi][:, 0:T], ALU.mult, [sgk, f"sgt16_{hi}"], [f"gatea{pp}{hi}"])
                    yield 10 + hi
                ws.done()

            def prelude_gen(hb, pp):
                for hi in range(2):
                    for src, dst, dk_ in ((kh16, khtok, "khtok"), (Va16, vtok, "vtok")):
                        for r in range((nch + 7) // 8):
                            ncr = min(8, nch - r * 8)

                            def trc(e, src=src, hi=hi, r=r, ncr=ncr):
                                for cc in range(ncr):
                                    c = r * 8 + cc
                                    ins = e.transpose(trp[0:32, cc * 128:(cc + 1) * 128], src[hi][:, c * 32:(c + 1) * 32], ident16[:])
                                return ins
                            S.op("pe", trc, [f"{'kh16' if src is kh16 else 'Va16'}_{hi}", "ident16"], ["trp"])
                            CP("act", dst[hi][:, r * 8:r * 8 + ncr, :],
                               trp[0:32, 0:ncr * 128].rearrange("p (c d) -> p c d", d=128), ["trp"], [f"{dk_}{hi}"])
                            yield

                    def mat(e, hi=hi):
                        for c in range(nch):
                            ins = e.matmul(atp[0:32, c * 32:(c + 1) * 32], lhsT=Kt16[hi][:, c * 32:(c + 1) * 32],
                                           rhs=Qt16[pp][hi][:, c * 32:(c + 1) * 32], start=True, stop=True)
                        return ins
                    S.op("pe", mat, [f"Kt16_{hi}", f"Qt16_{pp}{hi}"], ["atp"])
                    TT("dve", AT16[hi][:, 0:T].rearrange("p (c s) -> p c s", s=32), atp[0:32, 0:T].rearrange("p (c s) -> p c s", s=32),
                       mask32[:, :].unsqueeze(1).to_broadcast([32, nch, 32]), ALU.mult, ["atp", "mask32"], [f"AT16_{hi}"])
                    yield

            def chunkloop_gen(hb, pp):
                par = [0, 0]
                for bt in range(nch // 4):
                    for hi in range(2):
                        h = 2 * hb + hi

                        def mds(e, hi=hi, bt=bt):
                            for k in range(4):
                                c = bt * 4 + k
                                ins = e.matmul(dsp[:, k * 128:(k + 1) * 128], lhsT=khtok[hi][:, c, :], rhs=vtok[hi][:, c, :],
                                               start=True, stop=True)
                            return ins
                        S.op("pe", mds, [f"khtok{hi}", f"vtok{hi}"], ["dsp"])
                        for k in range(4):
                            c = bt * 4 + k
                            seg_s = 0 if not sample else (1 + c // 2)
                            seg_start = (c == 0) or (sample and c == 2)
                            seg_end = (c == nch - 1) or (sample and c == 1)
                            p = par[hi]
                            cur, ck = S32x[hi][p], f"S32_{hi}_{p}"
                            nxt, nk_ = S32x[hi][1 - p], f"S32_{hi}_{1 - p}"
                            if seg_start:
                                if seg_s == 0:
                                    if is_first_tile:
                                        MS("dve", cur[:], 0.0, [ck])
                                    else:
                                        LD(cur[:], stp_d[l, h], [ck], chan="io")
                                else:
                                    LD(cur[:], sth_d[l, seg_s - 1, h], [ck], chan="io")
                            CP("dve", S16q[hi][:, k, :], cur[:], [ck], [f"S16_{hi}"])
                            STT(nxt[:], cur[:], eend[pp][hi][:, c:c + 1], dsp[:, k * 128:(k + 1) * 128], ALU.mult, ALU.add,
                                [ck, f"eend{pp}{hi}", "dsp"], [nk_])
                            par[hi] = 1 - p
                            if seg_end:
                                dst = stp_d[l, h] if seg_s == 0 else sts_d[l, seg_s - 1, h]
                                ins = S.op("sp", lambda e, dst=dst, nxt=nxt: e.dma_start(out=dst, in_=nxt[:]), [nk_], (), dma="io")
                                if seg_s != 0 or is_last_prompt_tile:
                                    out_dmas.append(ins)
                        oslot = otp[:, hi * 128:(hi + 1) * 128]

                        def mo4(e, hi=hi, bt=bt, oslot=oslot):
                            for k in range(4):
                                c = bt * 4 + k
                                e.matmul(oslot[:, k * 32:(k + 1) * 32], lhsT=vtok[hi][:, c, :], rhs=AT16[hi][:, c * 32:(c + 1) * 32],
                                         start=True, stop=False)
                                ins = e.matmul(oslot[:, k * 32:(k + 1) * 32], lhsT=S16q[hi][:, k, :],
                                               rhs=Qt16[pp][hi][:, c * 32:(c + 1) * 32], start=False, stop=True)
                            return ins
                        S.op("pe", mo4, [f"vtok{hi}", f"AT16_{hi}", f"S16_{hi}", f"Qt16_{pp}{hi}"], [f"otp{hi}"])
                        CP("act", o32[hi][:, bt * 128:(bt + 1) * 128], oslot[:, 0:128], [f"otp{hi}"], [f"o32_{hi}"])
                        yield
                for hi in range(2):
                    h = 2 * hb + hi
                    ACT(sq16[hi][:, 0:T], o32[hi][:, 0:T], AF.Square, [f"o32_{hi}"], [f"sq16_{hi}"])
                    S.op("pe", lambda e, hi=hi: e.matmul(dnp[:, 0:T], lhsT=ones16[:], rhs=sq16[hi][:, 0:T], start=True, stop=True),
                         [f"sq16_{hi}", "ones16"], ["dnp"])
                    rs, rsk = next_scr()
                    ACT(rs[:, 0:T], dnp[:, 0:T], AF.Ln, ["dnp"], [rsk], scale=1.0 / 128, bias=EPS)
                    ACT(rs[:, 0:T], rs[:, 0:T], AF.Exp, [rsk], [rsk], scale=-0.5)
                    STT(rs[:, 0:T], o32[hi][:, 0:T], hg[:, l, h:h + 1], rs[:, 0:T], ALU.mult, ALU.mult,
                        [f"o32_{hi}", "hg", rsk], [rsk])
                    TT("dve", ua16[:, h, 0:T], rs[:, 0:T], gatea[pp][hi][:, 0:T], ALU.mult, [rsk, f"gatea{pp}{hi}"], [f"ua{h}"])
                    yield

            def attn_unit_g(g, c0, nq, kblocks, zero):
                for bi, (kc0, nk, vb, vblk) in enumerate(kblocks):
                    sps, spk = (atp, "atp") if bi == 0 else next_pj()

                    def ms(e, sps=sps, kc0=kc0, nk=nk):
                        for hl in range(4):
                            ins = e.matmul(sps[0:nk, hl * nq:(hl + 1) * nq], lhsT=kT[:, g, kc0:kc0 + nk],
                                           rhs=qT[:, 4 * g + hl, c0:c0 + nq], start=True, stop=True)
                        return ins
                    S.op("pe", ms, ["kT", "qT"], [spk])
                    ACT(PT[bi][0:nk, 0:4 * nq], sps[0:nk, 0:4 * nq], AF.Exp, [spk], [f"PT{bi}"], scale=0.125)
                    if zero:
                        pv = PT[bi][:, 0:512].rearrange("p (h q) -> p h q", q=128)
                        if vb == "prev":
                            MS("dve", pv[0:64, :, 64:128], 0.0, [f"PT{bi}"])
                        else:
                            MS("dve", pv[64:128, :, 0:64], 0.0, [f"PT{bi}"])
                yield

                def mpv(e):
                    for bi, (kc0, nk, vb, vblk) in enumerate(kblocks):
                        ins = e.matmul(dsp[:, 0:4 * nq], lhsT=vdup[0:nk, vblk, g, :], rhs=PT[bi][0:nk, 0:4 * nq],
                                       start=(bi == 0), stop=(bi == len(kblocks) - 1))
                    for bi, (kc0, nk, vb, vblk) in enumerate(kblocks):
                        ins = e.matmul(dnp[:, 0:4 * nq], lhsT=ones16[0:nk, :], rhs=PT[bi][0:nk, 0:4 * nq],
                                       start=(bi == 0), stop=(bi == len(kblocks) - 1))
                    return ins
                S.op("pe", mpv, [f"PT{bi}" for bi in range(len(kblocks))] + [f"vdup{kb[3]}" for kb in kblocks] + ["ones16"],
                     ["dsp", "dnp"])
                den, dk2 = next_scr()
                TT("dve", den[:, 0:4 * nq].rearrange("p (h q) -> p h q", q=nq),
                   dnp[:, 0:4 * nq].rearrange("p (h q) -> p h q", q=nq),
                   esink[:, l * 16 + 4 * g:l * 16 + 4 * g + 4].unsqueeze(2).to_broadcast([128, 4, nq]), ALU.add,
                   ["dnp", "esink"], [dk2])
                ACT(den[:, 0:4 * nq], den[:, 0:4 * nq], AF.Ln, [dk2], [dk2])
                ACT(den[:, 0:4 * nq], den[:, 0:4 * nq], AF.Exp, [dk2], [dk2], scale=-1.0)
                tmp, tk = next_scr()
                TT("dve", tmp[:, 0:4 * nq], dsp[:, 0:4 * nq], den[:, 0:4 * nq], ALU.mult, ["dsp", dk2], [tk])
                for half in range(2):
                    r0 = half * 64
                    tv = tmp[r0:r0 + 64, 0:4 * nq].rearrange("p (j x q) -> p j x q", x=2, q=nq)[:, :, half, :]
                    uv = ub16[r0:r0 + 64, 2 * g:2 * g + 2, c0:c0 + nq]
                    TT("dve", uv, tv, uv, ALU.mult, [tk, f"ub{2 * g}", f"ub{2 * g + 1}"], [f"ub{2 * g}", f"ub{2 * g + 1}"])
                yield

            def attn_gen(g):
                for qb in range(4):
                    kbl = []
                    if not (is_first_tile and qb == 0):
                        kbl.append((qb * 128, 128, "prev", qb))
                    kbl.append((128 + qb * 128, 128, "own", qb + 1))
                    yield from attn_unit_g(g, qb * 128, 128, kbl, True)

            def step(gen, n=1):
                if gen is None:
                    return None
                for _ in range(n):
                    try:
                        next(gen)
                    except StopIteration:
                        return None
                return gen

            def drain(gen):
                while gen is not None:
                    gen = step(gen)

            if not sample:
                core = None
                attn = []

                def step_attn(n):
                    for _ in range(n):
                        if attn:
                            attn[0] = step(attn[0], 1)
                            if attn[0] is None:
                                attn.pop(0)

                for hb in range(4):
                    pp = hb % 2
                    prel = None
                    for ci in fm_pair(hb, pp):
                        if ci < 6:
                            if ci % 2 == 1:
                                core = step(core, 4)
                                if ci < 4:
                                    step_attn(4)
                        else:
                            if prel is None:
                                drain(core)
                                core = None
                                prel = prelude_gen(hb, pp)
                            if ci % 2 == 1:
                                prel = step(prel, 4)
                    drain(prel)
                    core = chunkloop_gen(hb, pp)
                    attn.append(attn_gen(hb))
                while core is not None or attn:
                    core = step(core, 1)
                    if attn:
                        attn[0] = step(attn[0], 1)
                        if attn[0] is None:
                            attn.pop(0)
                CP("dve", khist[:, l, :, :], kT[:, :, 512:640], ["kT"], ["khist"])
                CP("dve", vhist[:, l, :, :], vdup[:, 4, :, :], ["vdup4"], ["vhist"])
            else:
                for hb in range(4):
                    pp = hb % 2
                    for _ in fm_pair(hb, pp):
                        pass
                    drain(prelude_gen(hb, pp))
                    drain(chunkloop_gen(hb, pp))
                for si in range(2):
                    LD(kv32[:, 0:256], cwk_d[l, si], ["kv32"], chan="io")
                    LD(kv32[:, 256:512], cwv_d[l, si], ["kv32"], chan="io")
                    CP("dve", k16[0][:, :], kv32[:, 0:256], ["kv32"], ["k16_0"])
                    CP("dve", vdup[:, 0, :, :].rearrange("p g (r d) -> p g r d", r=2),
                       kv32[:, 256:512].rearrange("p (g d) -> p g d", d=64).unsqueeze(2).to_broadcast([128, 4, 2, 64]),
                       ["kv32"], ["vdup0"])

                    def trk2(e):
                        for g in range(4):
                            ins = e.transpose(trp[0:64, g * 128:(g + 1) * 128], k16[0][:, g * 64:(g + 1) * 64], ident16[:])
                        return ins
                    S.op("pe", trk2, ["k16_0", "ident16"], ["trp"])
                    CP("dve", kT[:, :, 0:128], trp[0:64, 0:512].rearrange("p (g t) -> p g t", t=128), ["trp"], ["kT"])
                    for g in range(4):
                        drain(attn_unit_g(g, si * 64, 64, [(0, 128, "hist", 0), (128 + si * 64, 64, "own", 1 + si)], False))
            if STOP == "attn":
                DBG("ub", ub16[:].rearrange("p h t -> p (h t)"), [128, 8 * 512], [f"ub{h}" for h in range(8)])
            stop("attn")
            first_merge = True
            for j in range(16):
                w, wk = ws.get()
                wma = w[:, 0:2048].rearrange("p (k n) -> p k n", n=128)
                wmb = w[:, 2048:4096].rearrange("p (k n) -> p k n", n=128)
                wpa = w[:, 4096:5120].rearrange("p (k n) -> p k n", n=128)
                wpb = w[:, 5120:6144].rearrange("p (k n) -> p k n", n=128)
                t1 = None
                for which, wg, wp, u, ukeys in (("a", wma, wpa, ua16, [f"ua{h}" for h in range(8)]),
                                                ("b", wmb, wpb, ub16, [f"ub{h}" for h in range(8)])):
                    pj, pk = next_pj()

                    def mg(e, wg=wg, pj=pj):
                        for kc in range(KC):
                            ins = e.matmul(pj[:, 0:T], lhsT=wg[:, kc, :], rhs=hT[:, kc, 0:T], start=(kc == 0), stop=(kc == KC - 1))
                        return ins
                    S.op("pe", mg, [wk, "hT"], [pk])
                    sg_, sgk = sgt16[0 if which == "a" else 1], f"sgt16_{0 if which == 'a' else 1}"
                    ACT(sg_[:, 0:T], pj[:, 0:T], AF.Sigmoid, [pk], [sgk])
                    pj2, pk2 = next_pj()

                    def mp(e, wp=wp, pj2=pj2, u=u):
                        for kc in range(8):
                            ins = e.matmul(pj2[:, 0:T], lhsT=wp[:, kc, :], rhs=u[:, kc, 0:T], start=(kc == 0), stop=(kc == 7))
                        return ins
                    S.op("pe", mp, [wk] + ukeys, [pk2])
                    if which == "a":
                        t1, t1k = next_scr()
                        TT("dve", t1[:, 0:T], pj2[:, 0:T], sg_[:, 0:T], ALU.mult, [pk2, sgk], [t1k])
                    else:
                        t2, t2k = next_scr()
                        TT("dve", t2[:, 0:T], pj2[:, 0:T], sg_[:, 0:T], ALU.mult, [pk2, sgk], [t2k])
                        wr = [f"mg{j}"] + (["big16", "qT"] if first_merge else [])
                        rd_ = [t1k, t2k] + ([] if first_merge else ["big16"])
                        TT("dve", mg16[:, j, 0:T], t1[:, 0:T], t2[:, 0:T], ALU.add, rd_, wr)
                        first_merge = False
                ws.done()

            if STOP == "merge":
                DBG("mg", big16[:], [128, KC * 512], [f"mg{j}" for j in range(16)])
            stop("merge")
            pend_sq = []
            for gq in range(4):
                w, wk = ws.get()
                wv = w[:, :].rearrange("p (k n) -> p k n", n=512)
                for i in range(4):
                    j = gq * 4 + i
                    pj, pk = next_pj()

                    def mo2(e, wv=wv, pj=pj, i=i):
                        for kc in range(KC):
                            ins = e.matmul(pj[:, 0:T], lhsT=wv[:, kc, i * 128:(i + 1) * 128], rhs=mg16[:, kc, 0:T],
                                           start=(kc == 0), stop=(kc == KC - 1))
                        return ins
                    S.op("pe", mo2, [wk] + [f"mg{jj}" for jj in range(16)] + ["big16"], [pk])
                    for (c0, n, s) in segs:
                        STT(xT[:, j, c0:c0 + n], pj[:, c0:c0 + n], modT[:, l, 32 + j, s:s + 1], xT[:, j, c0:c0 + n],
                            ALU.mult, ALU.add, [pk, "modT", "xT"], ["xT"])
                    if pend_sq:
                        pend_sq.pop(0)()
                    sq, sk = sq16[j % 2], f"sq16_{j % 2}"
                    ACT(sq[:, 0:T], xT[:, j, 0:T], AF.Square, ["xT"], [sk])
                    pend_sq.append(lambda sq=sq, sk=sk, j=j: S.op(
                        "pe", lambda e: e.matmul(dnp[:, 0:T], lhsT=ones16[:], rhs=sq[:, 0:T], start=(j == 0), stop=(j == 15)),
                        [sk, "ones16"], ["dnp"]))
                ws.done()
            while pend_sq:
                pend_sq.pop(0)()
            S.op("dve", lambda e: e.memset(qT[0:1, 0, 0:1], 0.0), (), ["big16", "qT"] + [f"mg{jj}" for jj in range(16)])

        tok0 = 0
        for (ti, T) in tiles:
            if STOP == "mod":
                break
            LD(xT[:, :, 0:T], xT_d[:, :, tok0:tok0 + T].rearrange("k p t -> p k t"), ["xT"])
            if T == 512:
                LD(rope[:, 0:4, :], rope_d[:, ti * 4:(ti + 1) * 4, :], ["rope"])
            else:
                LD(rope[:, 0:1, :], rope_d[:, NTB - 1:NTB, :], ["rope"])
            try:
                for l in range(DEPTH):
                    layer_tile(ti, T, l, is_first_tile=(ti == 0), is_last_prompt_tile=(ti == NPT - 1))
            except _Stop:
                break
            rstd, rk = rms_rstd(lambda kc: xT[:, kc, 0:T], KC, T, ["xT"], 1.0 / D, pre=True)
            for kc in range(KC):
                y, yk = next_scr()
                STT(y[:, 0:T], xT[:, kc, 0:T], fng[:, kc:kc + 1], rstd[:, 0:T], ALU.mult, ALU.mult, ["xT", "fng", rk], [yk])
                ST(yT_d[kc, :, tok0:tok0 + T], y[:, 0:T], [yk])
            tok0 += T

        S.wait_all("sp", out_dmas)
        S.finalize()
        with nc.Block() as block:
            @block.tensor
            def _(e):
                S.replay("pe", e, sems)

            @block.scalar
            def _(e):
                S.replay("act", e, sems)

            @block.vector
            def _(e):
                S.replay("dve", e, sems)

            @block.gpsimd
            def _(e):
                S.replay("pool", e, sems)

            @block.sync
            def _(e):
                S.replay("sp", e, sems)
    return nc


def _grp(wcols):
    K, n = wcols.shape
    return np.ascontiguousarray(wcols.reshape(K // 128, 128, n).transpose(1, 0, 2)).reshape(128, -1)


def _pad(a):
    out = np.zeros((128, GW), np.float32)
    out[:, :a.shape[1]] = a
    return out


def _weight_stream(w_in, w_pa, w_pb, w_out, DEPTH):
    groups = []
    for l in range(DEPTH):
        wi = w_in[l]
        c = lambda name, i, n=128: wi[:, OFF[name] + i * n: OFF[name] + (i + 1) * n]
        groups.append(_grp(wi[:, OFF["qb"]:OFF["qb"] + 512]))
        groups.append(_grp(wi[:, OFF["qb"] + 512:OFF["qb"] + 1024]))
        groups.append(_grp(wi[:, OFF["kb"]:OFF["kb"] + 512]))
        for hb in range(4):
            h0, h1 = 2 * hb, 2 * hb + 1
            groups.append(_grp(np.concatenate([c("fa", h0), c("fa", h1), c("ia", h0), c("ia", h1)], axis=1)))
            groups.append(_grp(np.concatenate([c("qa", h0), c("qa", h1), c("za", h0), c("za", h1)], axis=1)))
            groups.append(_grp(np.concatenate([c("zb", h0), c("zb", h1), c("ga", h0), c("ga", h1)], axis=1)))
        for j in range(16):
            a = np.concatenate([_grp(c("ma", j)), _grp(c("mb", j)),
                                _grp(w_pa[l][:, j * 128:(j + 1) * 128]), _grp(w_pb[l][:, j * 128:(j + 1) * 128])], axis=1)
            groups.append(_pad(a))
        for gq in range(4):
            groups.append(_grp(w_out[l][:, gq * 512:(gq + 1) * 512]))
    return np.stack(groups)


def _rope_table(SEQ):
    NTB = SEQ // 128 + 1
    inv = (500000.0 ** (-np.arange(0, 16, 2) / 16)).astype(np.float32)
    tab = np.zeros((128, NTB, 32), np.float32)
    for tb in range(NTB):
        if tb < NTB - 1:
            pos = (tb * 128 + np.arange(128)).astype(np.float32)
        else:
            pos = (PAST + (np.arange(128) % 64)).astype(np.float32)
        ang = pos[:, None] * inv[None, :]
        cs, sn = np.cos(ang).astype(np.float32), np.sin(ang).astype(np.float32)
        tab[:, tb, 0:8] = cs
        tab[:, tb, 8:16] = cs
        tab[:, tb, 16:24] = -sn
        tab[:, tb, 24:32] = sn
    return tab


_NC_CACHE = {}


def kernel(x_prompt, x_sample, c_prompt, c_sample, state_hgrn, cache_win_k, cache_win_v,
           ada_w, ada_b, norm_g, w_in, lb_logits, hgrn_norm_g, sinks, w_branch_a, w_branch_b, w_out,
           final_norm_g):
    f = lambda a: np.asarray(a, dtype=np.float32)
    x_prompt, x_sample, c_prompt, c_sample = f(x_prompt), f(x_sample), f(c_prompt), f(c_sample)
    state_hgrn, cache_win_k, cache_win_v = f(state_hgrn), f(cache_win_k), f(cache_win_v)
    ada_w, ada_b, norm_g, w_in, lb_logits = f(ada_w), f(ada_b), f(norm_g), f(w_in), f(lb_logits)
    hgrn_norm_g, sinks, w_pa, w_pb, w_out, final_norm_g = f(hgrn_norm_g), f(sinks), f(w_branch_a), f(w_branch_b), f(w_out), f(final_norm_g)
    NCORES = x_prompt.shape[0]
    SEQ = x_prompt.shape[1]
    DEPTH = w_in.shape[0]
    NBUF = CFG["NBUF"]
    key = (SEQ, DEPTH, NBUF, STOP, SKIP)
    if key not in _NC_CACHE:
        _NC_CACHE[key] = build_nc(SEQ, DEPTH, NBUF)
    nc = _NC_CACHE[key]

    adaw = np.stack([_grp(ada_w[l][:, g * 512:(g + 1) * 512]) for l in range(DEPTH) for g in range(12)])
    wst = _weight_stream(w_in, w_pa, w_pb, w_out, DEPTH)
    adab = np.ascontiguousarray(ada_b.reshape(DEPTH, 48, 128).transpose(2, 0, 1))
    normg = np.ascontiguousarray(norm_g.reshape(DEPTH, KC, 128).transpose(2, 0, 1))
    lbl4 = np.zeros((4, 1024), np.float32) - 1e4
    lbl4[:DEPTH] = lb_logits
    lbl = np.ascontiguousarray(lbl4.reshape(4, 8, 128).transpose(2, 1, 0))
    hgl = np.ascontiguousarray(hgrn_norm_g.reshape(DEPTH, 8, 128).transpose(2, 0, 1))
    fng = np.ascontiguousarray(final_norm_g.reshape(KC, 128).T)
    rope = _rope_table(SEQ)
    ident = np.eye(128, dtype=np.float32).astype(ml_dtypes.bfloat16)
    m32 = (np.arange(32)[:, None] <= np.arange(32)[None, :]).astype(np.float32)
    mask32 = m32
    scanm = np.ones((128, 512), np.float32)
    scanm[:, ::32] = 0.0
    mrow = np.zeros((1, 4, 512), np.float32)
    mrow[0, 0, 0:64] = 1.0
    mrow[0, 1, 64:128] = 1.0
    qv = mrow[0, 2].reshape(4, 128)
    qv[:, 64:128] = -1000.0
    qv = mrow[0, 3].reshape(4, 128)
    qv[:, 0:64] = -1000.0
    shared = dict(adaw=adaw, wst=wst, adab=adab, normg=normg, lbl=lbl, hg=hgl, sinks=np.ascontiguousarray(sinks.reshape(-1)),
                  fng=fng, rope=rope, ident=ident, mask32=mask32.astype(ml_dtypes.bfloat16),
                  scanm=scanm.astype(ml_dtypes.bfloat16))
    in_maps = []
    for b in range(NCORES):
        xs = np.concatenate([x_prompt[b], x_sample[2 * b], x_sample[2 * b + 1]], axis=0)
        xT = np.ascontiguousarray(xs.T.reshape(KC, 128, -1))
        cs = np.stack([c_prompt[b], c_sample[2 * b], c_sample[2 * b + 1]], axis=1)
        cT = np.ascontiguousarray(cs.reshape(KC, 128, 3).transpose(1, 0, 2))
        m = dict(shared)
        m.update(xT=xT, cT=cT,
                 sth=np.ascontiguousarray(state_hgrn[:, 2 * b:2 * b + 2]),
                 cwk=np.ascontiguousarray(cache_win_k[:, 2 * b:2 * b + 2].reshape(DEPTH, 2, 128, 256)),
                 cwv=np.ascontiguousarray(cache_win_v[:, 2 * b:2 * b + 2].reshape(DEPTH, 2, 128, 256)))
        in_maps.append(m)
    res = run_bass_kernel_spmd(nc, in_maps, core_ids=list(range(NCORES)))
    R = res.results
    LAST["R"] = R
    y_prompt = np.stack([R[b]["yT"][:, :, :SEQ].reshape(D, SEQ).T for b in range(NCORES)])
    y_sample = np.stack([R[b]["yT"][:, :, SEQ + 64 * s:SEQ + 64 * (s + 1)].reshape(D, 64).T
                         for b in range(NCORES) for s in range(2)])
    stp = np.stack([R[b]["stp"] for b in range(NCORES)], axis=1)
    wkp = np.stack([R[b]["wkp"].reshape(DEPTH, 128, 4, 64) for b in range(NCORES)], axis=1)
    wvp = np.stack([R[b]["wvp"].reshape(DEPTH, 128, 4, 64) for b in range(NCORES)], axis=1)
    sts = np.concatenate([R[b]["sts"] for b in range(NCORES)], axis=1)
    nks = np.concatenate([R[b]["nks"].reshape(DEPTH, 2, 64, 4, 64) for b in range(NCORES)], axis=1)
    nvs = np.concatenate([R[b]["nvs"].reshape(DEPTH, 2, 64, 4, 64) for b in range(NCORES)], axis=1)
    o = lambda a: np.ascontiguousarray(a, dtype=np.float32)
    return (o(y_prompt), o(y_sample), o(stp), o(wkp), o(wvp), o(sts), o(nks), o(nvs))
```

```python
import contextlib
import numpy as np
import ml_dtypes
import concourse.bass as bass
import concourse.mybir as mybir
from concourse.bass_utils import run_bass_kernel_spmd

F32 = mybir.dt.float32
BF16 = mybir.dt.bfloat16
AF = mybir.ActivationFunctionType
ALU = mybir.AluOpType
AX = mybir.AxisListType

D = 2048
KC = 16
NIN = 11776
OFF = dict(qa=0, fa=1024, ia=2048, ga=3072, za=4096, qb=5120, kb=6144, vb=6400, zb=6656, ma=7680, mb=9728)
PAST = 4096
EPS = 1e-6
CFG = dict(SEQ=4096, DEPTH=4, NCORES=8, NBUF=2)
NGRP = 35
GW = 8192
import os
STOP = os.environ.get("KSTOP", "")
SKIP = os.environ.get("KSKIP", "")
LAST = {}


class _Stop(Exception):
    pass


class _Instr:
    __slots__ = ("chan", "idx", "needs_inc", "value", "fn", "waits", "eng", "is_dma")

    def __init__(self, chan, idx, fn, eng, is_dma):
        self.chan, self.idx, self.fn, self.eng, self.is_dma = chan, idx, fn, eng, is_dma
        self.needs_inc = is_dma
        self.value = None
        self.waits = []


class Sched:
    ENGS = ("pe", "act", "dve", "pool", "sp")

    def __init__(self):
        self.lists = {e: [] for e in self.ENGS}
        self.chan_n, self.chan_last = {}, {}
        self.seen = {e: {} for e in self.ENGS}
        self.lw, self.rd = {}, {}
        self.dma_chans = []

    def _dep(self, eng, ins, p):
        if p is None or p is ins:
            return
        if p.chan == eng and eng == "pe":
            return
        s = self.seen[eng]
        if s.get(p.chan, -1) >= p.idx:
            return
        s[p.chan] = p.idx
        p.needs_inc = True
        ins.waits.append(p)

    PSUM_BANK = {"pj0": "pj0", "pj1": "pj1", "pj2": "pj2", "trp": "trp", "atp": "atp", "otp": "otp", "otp0": "otp",
                 "otp1": "otp", "dsp": "dsp", "dsp0": "dsp", "dsp1": "dsp", "dsp2": "dsp", "dsp3": "dsp", "dnp": "dnp"}

    def op(self, eng, fn, reads=(), writes=(), dma=None):
        pb = self.PSUM_BANK
        banks = [pb[k] for k in list(reads) + list(writes) if k in pb]
        reads = [k for k in reads if k not in pb]
        writes = [k for k in writes if k not in pb] + sorted(set(banks))
        chan = dma if dma is not None else eng
        if dma is not None and chan not in self.chan_n:
            self.dma_chans.append(chan)
        idx = self.chan_n.get(chan, 0)
        self.chan_n[chan] = idx + 1
        ins = _Instr(chan, idx, fn, eng, dma is not None)
        if dma is not None:
            self._dep(eng, ins, self.chan_last.get(chan))
        self.chan_last[chan] = ins
        for k in reads:
            self._dep(eng, ins, self.lw.get(k))
        for k in writes:
            self._dep(eng, ins, self.lw.get(k))
            for r in self.rd.get(k, ()):
                self._dep(eng, ins, r)
        for k in writes:
            self.lw[k] = ins
            self.rd[k] = []
        for k in reads:
            if k not in writes:
                self.rd.setdefault(k, []).append(ins)
        self.lists[eng].append(ins)
        return ins

    def wait_all(self, eng, instrs):
        ins = _Instr(eng, -1, None, eng, False)
        for p in instrs:
            self._dep(eng, ins, p)
        self.lists[eng].append(ins)

    def finalize(self):
        for eng in self.ENGS:
            c = 0
            for ins in self.lists[eng]:
                if ins.fn is None:
                    continue
                if ins.is_dma:
                    ins.value = 16 * (ins.idx + 1)
                elif ins.needs_inc:
                    c += 1
                    ins.value = c

    def replay(self, eng, handle, sems):
        for ins in self.lists[eng]:
            for p in ins.waits:
                handle.wait_ge(sems[p.chan], p.value)
            if ins.fn is None:
                continue
            r = ins.fn(handle)
            if ins.is_dma:
                r.then_inc(sems[ins.chan], 16)
            elif ins.needs_inc:
                r.then_inc(sems[ins.chan], 1)


def build_nc(SEQ, DEPTH, NBUF):
    NPT = SEQ // 512
    NTOK = SEQ + 128
    NTB = SEQ // 128 + 1
    nc = bass.Bass("TRN2", target_bir_lowering=False)
    S = Sched()

    def din(name, shape, dt=F32):
        return nc.dram_tensor(name, list(shape), dt, kind="ExternalInput").ap()

    def dout(name, shape):
        return nc.dram_tensor(name, list(shape), F32, kind="ExternalOutput").ap()

    xT_d = din("xT", [KC, 128, NTOK])
    cT_d = din("cT", [128, KC, 3])
    adaw_d = din("adaw", [DEPTH * 12, 128, GW])
    adab_d = din("adab", [128, DEPTH, 48])
    normg_d = din("normg", [128, DEPTH, KC])
    lbl_d = din("lbl", [128, 8, 4])
    hg_d = din("hg", [128, DEPTH, 8])
    sinks_d = din("sinks", [DEPTH * 16])
    fng_d = din("fng", [128, KC])
    rope_d = din("rope", [128, NTB, 32])
    ident_d = din("ident", [128, 128], BF16)
    mask32_d = din("mask32", [32, 32], BF16)
    scanm_d = din("scanm", [128, 512], BF16)
    wst_d = din("wst", [DEPTH * NGRP, 128, GW])
    sth_d = din("sth", [DEPTH, 2, 8, 128, 128])
    cwk_d = din("cwk", [DEPTH, 2, 128, 256])
    cwv_d = din("cwv", [DEPTH, 2, 128, 256])

    yT_d = dout("yT", [KC, 128, NTOK])
    stp_d = dout("stp", [DEPTH, 8, 128, 128])
    wkp_d = dout("wkp", [DEPTH, 128, 256])
    wvp_d = dout("wvp", [DEPTH, 128, 256])
    sts_d = dout("sts", [DEPTH, 2, 8, 128, 128])
    nks_d = dout("nks", [DEPTH, 2, 64, 256])
    nvs_d = dout("nvs", [DEPTH, 2, 64, 256])

    es = contextlib.ExitStack()
    with es:
        def sb(name, shape, dt=F32):
            return es.enter_context(nc.sbuf_tensor(name, list(shape), dt))

        def ps(name, shape, dt=F32):
            return es.enter_context(nc.psum_tensor(name, list(shape), dt))

        wbuf = [sb(f"wbuf{i}", [128, GW], BF16) for i in range(NBUF)]
        xT = sb("xTs", [128, KC, 512])
        hT = sb("hT", [128, KC, 512], BF16)
        big16 = sb("big16", [128, KC * 512], BF16)
        qT = big16[0:64, :].rearrange("p (h t) -> p h t", t=512)
        mg16 = big16[:, :].rearrange("p (k t) -> p k t", t=512)
        NSCR = 3
        scr = [sb(f"scr{i}", [128, 512]) for i in range(NSCR)]
        qtok = [sb(f"qtok{i}", [128, 512], BF16) for i in range(2)]
        kv32 = sb("kv32", [128, 512])
        k16 = [sb(f"k16_{i}", [128, 256], BF16) for i in range(2)]
        vdup = sb("vdup", [128, 5, 4, 128], BF16)
        kT = sb("kT", [64, 4, 640], BF16)
        khist = sb("khist", [64, DEPTH, 4, 128], BF16)
        vhist = sb("vhist", [128, DEPTH, 4, 128], BF16)
        ua16 = sb("ua16", [128, 8, 512], BF16)
        ub16 = sb("ub16", [128, 8, 512], BF16)
        gatea = [[sb(f"gatea{p}{i}", [128, 512], BF16) for i in range(2)] for p in range(2)]
        Qt16 = [[sb(f"Qt16_{p}{i}", [128, 512], BF16) for i in range(2)] for p in range(2)]
        Kt16 = [sb(f"Kt16_{i}", [128, 512], BF16) for i in range(2)]
        kh16 = [sb(f"kh16_{i}", [128, 512], BF16) for i in range(2)]
        Va16 = [sb(f"Va16_{i}", [128, 512], BF16) for i in range(2)]
        khtok = [sb(f"khtok{i}", [32, 16, 128], BF16) for i in range(2)]
        vtok = [sb(f"vtok{i}", [32, 16, 128], BF16) for i in range(2)]
        AT16 = [sb(f"AT16_{i}", [32, 512], BF16) for i in range(2)]
        eend = [[sb(f"eend{p}{i}", [128, 16]) for i in range(2)] for p in range(2)]
        o32 = [sb(f"o32_{i}", [128, 512]) for i in range(2)]
        S32x = [[sb(f"S32_{i}_{p}", [128, 128]) for p in range(2)] for i in range(2)]
        S16q = [sb(f"S16q_{i}", [128, 4, 128], BF16) for i in range(2)]
        PT = [sb(f"PT{i}", [128, 512], BF16) for i in range(2)]
        sq16 = [sb(f"sq16_{i}", [128, 512], BF16) for i in range(2)]
        sgt16 = [sb(f"sgt16_{i}", [128, 512], BF16) for i in range(2)]
        faS = [[sb(f"faS{i}{k}", [128, 512]) for k in range(3)] for i in range(2)]
        rstdS = sb("rstdS", [128, 512])
        ident16 = sb("ident16", [128, 128], BF16)
        ones16 = sb("ones16", [128, 128], BF16)
        mask32 = sb("mask32s", [32, 32], BF16)
        scanm = sb("scanms", [128, 512], BF16)
        rope = sb("ropes", [128, 4, 32])
        cT16 = sb("cT16", [128, KC, 3], BF16)
        modT = sb("modT", [128, DEPTH, 48, 3])
        Gm = sb("Gm", [128, DEPTH, KC, 3])
        fng = sb("fngs", [128, KC])
        lbv = sb("lbv", [128, 8, 4])
        lb1 = sb("lb1", [128, 8, 4])
        nlb1 = sb("nlb1", [128, 8, 4])
        hg = sb("hgs", [128, DEPTH, 8])
        esink = sb("esink", [128, DEPTH * 16])

        _o = o32[0]
        adab = _o[:, 0:DEPTH * 48].rearrange("p (l c) -> p l c", c=48)
        normg = _o[:, 192:192 + DEPTH * KC].rearrange("p (l c) -> p l c", c=KC)
        lbl = _o[:, 256:288].rearrange("p (h l) -> p h l", l=4)
        lbe = _o[:, 288:320].rearrange("p (h l) -> p h l", l=4)
        lbs = _o[:, 320:328]
        cT32 = _o[:, 328:376].rearrange("p (k s) -> p k s", s=3)
        PJ = [ps(f"pj{i}", [128, 512]) for i in range(3)]
        trp = ps("trp", [128, 1024], BF16)
        atp = ps("atp", [128, 512])
        otp = ps("otp", [128, 512])
        dsp = ps("dsp", [128, 512])
        dnp = ps("dnp", [128, 512])

        sem_names = list(S.ENGS) + ["ld", "st", "st2", "io"] + [f"w{i}" for i in range(NBUF)]
        sems = {n: es.enter_context(nc.semaphore(n)) for n in sem_names}

        cnt = {"pj": 0, "scr": 0, "st": 0}

        def next_pj():
            i = cnt["pj"] % 3
            cnt["pj"] += 1
            return PJ[i], f"pj{i}"

        def next_scr():
            i = cnt["scr"] % NSCR
            cnt["scr"] += 1
            return scr[i], f"scr{i}"

        def A(eng, out, in_, func, reads, writes, **kw):
            S.op(eng, lambda e: e.activation(out=out, in_=in_, func=func, **kw), reads, writes)

        def ACT(out, in_, func, reads, writes, **kw):
            A("act", out, in_, func, reads, writes, **kw)

        def TT(eng, out, in0, in1, op, reads, writes):
            S.op(eng, lambda e: e.tensor_tensor(out=out, in0=in0, in1=in1, op=op), reads, writes)

        def TS(eng, out, in0, s1, s2, op0, op1, reads, writes):
            S.op(eng, lambda e: e.tensor_scalar(out=out, in0=in0, scalar1=s1, scalar2=s2, op0=op0, op1=op1), reads, writes)

        def STT(out, in0, scalar, in1, op0, op1, reads, writes):
            S.op("dve", lambda e: e.scalar_tensor_tensor(out=out, in0=in0, scalar=scalar, in1=in1, op0=op0, op1=op1), reads, writes)

        def CP(eng, out, in_, reads, writes):
            if eng == "act":
                ACT(out, in_, AF.Copy, reads, writes)
            else:
                S.op(eng, lambda e: e.tensor_copy(out=out, in_=in_), reads, writes)

        def MS(eng, ap, val, writes):
            S.op(eng, lambda e: e.memset(ap, val), (), writes)

        def LD(out, in_, writes, chan="ld"):
            return S.op("sp", lambda e: e.dma_start(out=out, in_=in_), (), writes, dma=chan)

        out_dmas = []

        def ST(out, in_, reads, extra_writes=()):
            chan = ("st", "st2")[cnt["st"] % 2]
            cnt["st"] += 1
            ins = S.op("sp", lambda e: e.dma_start(out=out, in_=in_), reads, extra_writes, dma=chan)
            out_dmas.append(ins)
            return ins

        def DBG(name, ap, shape, reads):
            d = nc.dram_tensor("dbg_" + name, list(shape), ap.dtype, kind="ExternalOutput").ap()
            ST(d, ap, reads)

        def stop(phase):
            if STOP == phase:
                raise _Stop()

        class WS:
            def __init__(self):
                self.plan = []
                self.nissued = 0
                self.nget = 0

            def add(self, ap, ncols=GW):
                self.plan.append((ap, ncols))

            def _issue(self):
                if self.nissued >= len(self.plan):
                    return
                i = self.nissued
                b = i % NBUF
                ap, ncols = self.plan[i]
                S.op("pool", lambda e: e.dma_start(out=wbuf[b][:, 0:ncols], in_=ap[:, 0:ncols]),
                     (), [f"wbuf{b}"], dma=f"w{b}")
                self.nissued += 1

            def start(self):
                for _ in range(NBUF):
                    self._issue()

            def get(self):
                b = self.nget % NBUF
                assert self.nget < self.nissued
                self.nget += 1
                return wbuf[b], f"wbuf{b}"

            def done(self):
                self._issue()

        ws = WS()
        for g in range(DEPTH * 12):
            ws.add(adaw_d[g])
        tiles = [(t, 512) for t in range(NPT)] + [(NPT, 128)]
        for (ti, T) in tiles:
            for l in range(DEPTH):
                for g in range(NGRP):
                    ncols = 6144 if 15 <= g < 31 else GW
                    ws.add(wst_d[l * NGRP + g], ncols)

        LD(ident16[:], ident_d, ["ident16"])
        LD(mask32[:], mask32_d, ["mask32"])
        LD(scanm[:], scanm_d, ["scanm"])
        LD(cT32[:], cT_d, ["cT32"])
        LD(adab[:], adab_d, ["adab"])
        LD(normg[:], normg_d, ["normg"])
        LD(fng[:], fng_d, ["fng"])
        LD(lbl[:], lbl_d, ["lbl"])
        LD(hg[:], hg_d, ["hg"])
        LD(esink[:], sinks_d.partition_broadcast(128), ["esink"])
        ws.start()
        MS("dve", ones16[:], 1.0, ["ones16"])
        CP("dve", cT16[:], cT32[:], ["cT32"], ["cT16"])
        ACT(esink[:], esink[:], AF.Exp, ["esink"], ["esink"])
        ACT(lbe[:], lbl[:], AF.Exp, ["lbl"], ["lbe"])
        S.op("dve", lambda e: e.tensor_reduce(out=lbs[:], in_=lbe[:], axis=AX.X, op=ALU.add), ["lbe"], ["lbs"])
        S.op("dve", lambda e: e.reciprocal(out=lbs[:], in_=lbs[:]), ["lbs"], ["lbs"])
        TT("dve", lbe[:], lbe[:], lbs[:].unsqueeze(2).to_broadcast([128, 8, 4]), ALU.mult, ["lbe", "lbs"], ["lbe"])
        MS("dve", lbv[:], 0.0, ["lbv"])
        for l in range(1, 4):
            TT("dve", lbv[:, :, l:l + 1], lbv[:, :, l - 1:l], lbe[:, :, l:l + 1], ALU.add, ["lbv", "lbe"], ["lbv"])
        TS("dve", lb1[:], lbv[:], -1.0, 1.0, ALU.mult, ALU.add, ["lbv"], ["lb1"])
        TS("dve", nlb1[:], lbv[:], 1.0, -1.0, ALU.mult, ALU.add, ["lbv"], ["nlb1"])

        for l in range(DEPTH):
            for g in range(12):
                w, wk = ws.get()
                wv = w[:, :].rearrange("p (k n) -> p k n", n=512)
                pj, pk = next_pj()

                def mm(e, wv=wv, pj=pj):
                    for i in range(4):
                        for kc in range(KC):
                            ins = e.matmul(pj[:, i * 4:i * 4 + 3], lhsT=wv[:, kc, i * 128:(i + 1) * 128],
                                           rhs=cT16[:, kc, :], start=(kc == 0), stop=(kc == KC - 1))
                    return ins
                S.op("pe", mm, [wk, "cT16"], [pk])
                ws.done()
                TT("dve", modT[:, l, g * 4:(g + 1) * 4, :], pj[:, 0:16].rearrange("p (i s) -> p i s", s=4)[:, :, 0:3],
                   adab[:, l, g * 4:(g + 1) * 4].unsqueeze(2).to_broadcast([128, 4, 3]), ALU.add,
                   [pk, "adab"], ["modT"])
        for l in range(DEPTH):
            TS("dve", Gm[:, l, :, :], modT[:, l, 16:32, :], 1.0, None, ALU.add, ALU.bypass, ["modT"], ["Gm"])
            TT("dve", Gm[:, l, :, :], Gm[:, l, :, :], normg[:, l, :].unsqueeze(2).to_broadcast([128, KC, 3]),
               ALU.mult, ["Gm", "normg"], ["Gm"])

        if STOP == "mod":
            DBG("modT", modT[:].rearrange("p l c s -> p (l c s)"), [128, DEPTH * 48 * 3], ["modT"])
            DBG("Gm", Gm[:].rearrange("p l c s -> p (l c s)"), [128, DEPTH * KC * 3], ["Gm"])
            DBG("lbv", lbv[:].rearrange("p h l -> p (h l)"), [128, 32], ["lbv"])
            DBG("esink", esink[:], [128, DEPTH * 16], ["esink"])
        def rms_rstd(src_fn, nchunk, T, key_reads, scale, pre=False):
            for kc in range(0 if pre else nchunk):
                sq, sk = sq16[kc % 2], f"sq16_{kc % 2}"
                ACT(sq[:, 0:T], src_fn(kc), AF.Square, key_reads, [sk])
                S.op("pe", lambda e, sq=sq, kc=kc: e.matmul(dnp[:, 0:T], lhsT=ones16[:], rhs=sq[:, 0:T],
                                                           start=(kc == 0), stop=(kc == nchunk - 1)),
                     [sk, "ones16"], ["dnp"])
            ln, lk = rstdS, "rstdS"
            ACT(ln[:, 0:T], dnp[:, 0:T], AF.Ln, ["dnp"], [lk], scale=scale, bias=EPS)
            ACT(ln[:, 0:T], ln[:, 0:T], AF.Exp, [lk], [lk], scale=-0.5)
            return ln, lk

        def segs_of(T):
            return [(0, 512, 0)] if T == 512 else [(0, 64, 1), (64, 64, 2)]

        def layer_tile(ti, T, l, is_first_tile, is_last_prompt_tile):
            sample = (T == 128)
            segs = segs_of(T)
            blocks = [(i * 128, 128, 0) for i in range(4)] if not sample else [(0, 64, 1), (64, 64, 2)]
            nch = T // 32
            rstd, rk = rms_rstd(lambda kc: xT[:, kc, 0:T], KC, T, ["xT"], 1.0 / D, pre=(l > 0))
            for kc in range(KC):
                tmp, tk = next_scr()
                for (c0, n, s) in segs:
                    STT(tmp[:, c0:c0 + n], xT[:, kc, c0:c0 + n], Gm[:, l, kc, s:s + 1], rstd[:, c0:c0 + n],
                        ALU.mult, ALU.mult, ["xT", "Gm", rk], [tk])
                    ACT(hT[:, kc, c0:c0 + n], tmp[:, c0:c0 + n], AF.Identity, [tk, "modT"], ["hT"],
                        bias=modT[:, l, kc, s:s + 1])

            if STOP == "norm":
                DBG("hT", hT[:].rearrange("p k t -> p (k t)"), [128, KC * 512], ["hT"])
            stop("norm")
            def rope_fix(pj, pk, n, nh, slot, out_view, out_key):
                if "r" in SKIP:
                    return
                pv = pj[0:n, 0:nh * 64].rearrange("p (h d) -> p h d", d=64)
                t1, k1 = next_scr()
                t2, k2 = next_scr()
                t1v = t1[0:n, 0:nh * 16].rearrange("p (h d) -> p h d", d=16)
                t2v = t2[0:n, 0:nh * 16].rearrange("p (h d) -> p h d", d=16)
                TT("dve", t1v, pv[:, :, 0:16], rope[0:n, slot, 0:16].unsqueeze(1).to_broadcast([n, nh, 16]),
                   ALU.mult, [pk, "rope"], [k1])
                TT("dve", t2v[:, :, 0:8], pv[:, :, 8:16], rope[0:n, slot, 16:24].unsqueeze(1).to_broadcast([n, nh, 8]),
                   ALU.mult, [pk, "rope"], [k2])
                TT("dve", t2v[:, :, 8:16], pv[:, :, 0:8], rope[0:n, slot, 24:32].unsqueeze(1).to_broadcast([n, nh, 8]),
                   ALU.mult, [pk, "rope"], [k2])
                TT("dve", out_view, t1v, t2v, ALU.add, [k1, k2], [out_key])

            if not sample and not is_first_tile:
                CP("dve", kT[:, :, 0:128], khist[:, l, :, :], ["khist"], ["kT"])
                CP("dve", vdup[:, 0, :, :], vhist[:, l, :, :], ["vhist"], ["vdup0"])
            pend_tr = []
            for gi in range(3):
                w, wk = ws.get()
                wv = w[:, :].rearrange("p (k n) -> p k n", n=512)
                for bi, (c0, n, s) in enumerate(blocks):
                    slot = bi if not sample else 0
                    pj, pk = next_pj()

                    def mm(e, wv=wv, pj=pj, c0=c0, n=n):
                        for kc in range(KC):
                            ins = e.matmul(pj[0:n, :], lhsT=hT[:, kc, c0:c0 + n], rhs=wv[:, kc, :],
                                           start=(kc == 0), stop=(kc == KC - 1))
                        return ins
                    S.op("pe", mm, [wk, "hT"], [pk])
                    while pend_tr:
                        pend_tr.pop(0)()
                    rb = bi % 2
                    if gi < 2:
                        qt_, qk_ = qtok[rb], f"qtok{rb}"
                        ACT(qt_[0:n, 0:512], pj[0:n, :], AF.Copy, [pk], [qk_])
                        rope_fix(pj, pk, n, 8, slot,
                                 qt_[0:n, 0:512].rearrange("p (h d) -> p h d", d=64)[:, :, 0:16], qk_)

                        def do_trq(qt_=qt_, qk_=qk_, n=n, c0=c0, gi=gi, bi=bi):
                            def trq(e):
                                for hh in range(8):
                                    ins = e.transpose(trp[0:64, hh * 128:hh * 128 + n], qt_[0:n, hh * 64:(hh + 1) * 64],
                                                      ident16[0:n, 0:n])
                                return ins
                            S.op("pe", trq, [qk_, "ident16"], ["trp"])
                            CP("dve" if bi % 2 == 0 else "act", qT[:, gi * 8:(gi + 1) * 8, c0:c0 + n],
                               trp[0:64, :].rearrange("p (h t) -> p h t", t=128)[:, :, 0:n], ["trp", "big16"], ["qT"])
                        pend_tr.append(do_trq)
                    else:
                        k16_, k16k = k16[rb], f"k16_{rb}"
                        ACT(kv32[0:n, :], pj[0:n, :], AF.Copy, [pk], ["kv32"])
                        rope_fix(pj, pk, n, 4, slot,
                                 kv32[0:n, 0:256].rearrange("p (h d) -> p h d", d=64)[:, :, 0:16], "kv32")
                        CP("dve", k16_[0:n, :], kv32[0:n, 0:256], ["kv32"], [k16k])
                        CP("dve", vdup[0:n, 1 + bi, :, :].rearrange("p g (r d) -> p g r d", r=2),
                           kv32[0:n, 256:512].rearrange("p (g d) -> p g d", d=64).unsqueeze(2).to_broadcast([n, 4, 2, 64]),
                           ["kv32"], [f"vdup{1 + bi}"])
                        if sample:
                            ST(nks_d[l, s - 1], kv32[0:n, 0:256], ["kv32"])
                            ST(nvs_d[l, s - 1], kv32[0:n, 256:512], ["kv32"])
                        elif is_last_prompt_tile and bi == 3:
                            ST(wkp_d[l], kv32[0:n, 0:256], ["kv32"])
                            ST(wvp_d[l], kv32[0:n, 256:512], ["kv32"])

                        def do_trk(k16_=k16_, k16k=k16k, n=n, c0=c0):
                            def trk(e):
                                for g in range(4):
                                    ins = e.transpose(trp[0:64, g * 128:g * 128 + n], k16_[0:n, g * 64:(g + 1) * 64], ident16[0:n, 0:n])
                                return ins
                            S.op("pe", trk, [k16k, "ident16"], ["trp"])
                            CP("dve", kT[:, :, 128 + c0:128 + c0 + n], trp[0:64, 0:512].rearrange("p (g t) -> p g t", t=128)[:, :, 0:n],
                               ["trp"], ["kT"])
                        pend_tr.append(do_trk)
                ws.done()
            while pend_tr:
                pend_tr.pop(0)()

            if STOP == "tm":
                DBG("qT", big16[0:64, :], [64, KC * 512], ["qT"])
                DBG("kT", kT[:, :, 128:640], [64, 4, 512], ["kT"])
                DBG("vdup", vdup[:, 1:5].rearrange("p b g d -> p (b g d)"), [128, 4 * 4 * 128], [f"vdup{i}" for i in range(5)])
            stop("tm")
            def fm_chunk(wv, wk, i):
                pj, pk = next_pj()

                def mm(e, wv=wv, pj=pj, i=i):
                    for kc in range(KC):
                        ins = e.matmul(pj[:, 0:T], lhsT=wv[:, kc, i * 128:(i + 1) * 128], rhs=hT[:, kc, 0:T],
                                       start=(kc == 0), stop=(kc == KC - 1))
                    return ins
                S.op("pe", mm, [wk, "hT"], [pk])
                return pj, pk

            def fm_pair(hb, pp):
                tails = []
                w, wk = ws.get()
                wv = w[:, :].rearrange("p (k n) -> p k n", n=512)
                for hi in range(2):
                    h = 2 * hb + hi
                    pj, pk = fm_chunk(wv, wk, hi)
                    E, ek = faS[hi][0], f"faE{hi}"
                    L1, l1k = faS[hi][1], f"faL{hi}"
                    L2, l2k = faS[hi][2], f"faM{hi}"
                    ACT(E[:, 0:T], pj[:, 0:T], AF.Exp, [pk], [ek], scale=-1.0)
                    ACT(L1[:, 0:T], E[:, 0:T], AF.Ln, [ek], [l1k], bias=1.0)
                    ACT(E[:, 0:T], L1[:, 0:T], AF.Exp, [l1k], [ek], scale=-1.0)
                    ACT(L2[:, 0:T], E[:, 0:T], AF.Ln, [ek, "lbv", "lb1"], [l2k], scale=lb1[:, h, l:l + 1],
                        bias=lbv[:, h, l:l + 1])
                    S.op("dve", lambda e, L1=L1, L2=L2: e.tensor_tensor_scan(out=L1[:, 0:T], data0=scanm[:, 0:T], data1=L2[:, 0:T],
                                                                          initial=0.0, op0=ALU.mult, op1=ALU.add),
                         ["scanm", l2k], [l1k])
                    TS("dve", E[:, 0:T], E[:, 0:T], nlb1[:, h, l:l + 1], lb1[:, h, l:l + 1], ALU.mult, ALU.add,
                       [ek, "nlb1", "lb1"], [ek])

                    def tail(hi=hi, E=E, ek=ek, L1=L1, l1k=l1k, L2=L2, l2k=l2k):
                        enb, enk = next_scr()
                        ACT(L2[:, 0:T], L1[:, 0:T], AF.Exp, [l1k], [l2k])
                        ACT(enb[:, 0:T], L1[:, 0:T], AF.Exp, [l1k], [enk], scale=-1.0)
                        TT("dve", Kt16[hi][:, 0:T], E[:, 0:T], enb[:, 0:T], ALU.mult, [ek, enk], [f"Kt16_{hi}"])
                        ebv = L2[:, 0:T].rearrange("p (c s) -> p c s", s=32)
                        TT("dve", kh16[hi][:, 0:T].rearrange("p (c s) -> p c s", s=32),
                           Kt16[hi][:, 0:T].rearrange("p (c s) -> p c s", s=32),
                           ebv[:, :, 31:32].to_broadcast([128, nch, 32]), ALU.mult, [f"Kt16_{hi}", l2k], [f"kh16_{hi}"])
                        CP("dve", eend[pp][hi][:, 0:nch], ebv[:, :, 31], [l2k], [f"eend{pp}{hi}"])
                    tails.append(tail)
                    if hi == 1:
                        tails.pop(0)()
                    yield hi
                for hi in range(2):
                    pj, pk = fm_chunk(wv, wk, 2 + hi)
                    ACT(Va16[hi][:, 0:T], pj[:, 0:T], AF.Copy, [pk], [f"Va16_{hi}"])
                    if hi == 0:
                        tails.pop(0)()
                    yield 2 + hi
                ws.done()
                w, wk = ws.get()
                wv = w[:, :].rearrange("p (k n) -> p k n", n=512)
                for hi in range(2):
                    pj, pk = fm_chunk(wv, wk, hi)
                    q, qk = next_scr()
                    ACT(q[:, 0:T], pj[:, 0:T], AF.Silu, [pk], [qk])
                    TT("dve", Qt16[pp][hi][:, 0:T], q[:, 0:T], faS[hi][2][:, 0:T], ALU.mult, [qk, f"faM{hi}"], [f"Qt16_{pp}{hi}"])
                    yield 4 + hi
                for hi in range(2):
                    pj, pk = fm_chunk(wv, wk, 2 + hi)
                    ACT(sgt16[hi][:, 0:T], pj[:, 0:T], AF.Silu, [pk], [f"sgt16_{hi}"])
                    yield 6 + hi
                ws.done()
                w, wk = ws.get()
                wv = w[:, :].rearrange("p (k n) -> p k n", n=512)
                for jj in range(2):
                    j = 2 * hb + jj
                    pj, pk = fm_chunk(wv, wk, jj)
                    ACT(ub16[:, j, 0:T], pj[:, 0:T], AF.Silu, [pk], [f"ub{j}"])
                    yield 8 + jj
                for hi in range(2):
                    pj, pk = fm_chunk(wv, wk, 2 + hi)
                    sg_, sgk = next_scr()
                    ACT(sg_[:, 0:T], pj[:, 0:T], AF.Sigmoid, [pk], [sgk])
                    TT("dve", gatea[pp][hi][:, 0:T], sg_[:, 0:T], sgt16[hi][:, 0:T], ALU.mult, [sgk, f"sgt16_{hi}"], [f"gatea{pp}{hi}"])
                    yield 10 + hi
                ws.done()

            def prelude_gen(hb, pp):
                for hi in range(2):
                    for src, dst, dk_ in ((kh16, khtok, "khtok"), (Va16, vtok, "vtok")):
                        for r in range((nch + 7) // 8):
                            ncr = min(8, nch - r * 8)

                            def trc(e, src=src, hi=hi, r=r, ncr=ncr):
                                for cc in range(ncr):
                                    c = r * 8 + cc
                                    ins = e.transpose(trp[0:32, cc * 128:(cc + 1) * 128], src[hi][:, c * 32:(c + 1) * 32], ident16[:])
                                return ins
                            S.op("pe", trc, [f"{'kh16' if src is kh16 else 'Va16'}_{hi}", "ident16"], ["trp"])
                            CP("act", dst[hi][:, r * 8:r * 8 + ncr, :],
                               trp[0:32, 0:ncr * 128].rearrange("p (c d) -> p c d", d=128), ["trp"], [f"{dk_}{hi}"])
                            yield

                    def mat(e, hi=hi):
                        for c in range(nch):
                            ins = e.matmul(atp[0:32, c * 32:(c + 1) * 32], lhsT=Kt16[hi][:, c * 32:(c + 1) * 32],
                                           rhs=Qt16[pp][hi][:, c * 32:(c + 1) * 32], start=True, stop=True)
                        return ins
                    S.op("pe", mat, [f"Kt16_{hi}", f"Qt16_{pp}{hi}"], ["atp"])
                    TT("dve", AT16[hi][:, 0:T].rearrange("p (c s) -> p c s", s=32), atp[0:32, 0:T].rearrange("p (c s) -> p c s", s=32),
                       mask32[:, :].unsqueeze(1).to_broadcast([32, nch, 32]), ALU.mult, ["atp", "mask32"], [f"AT16_{hi}"])
                    yield

            def chunkloop_gen(hb, pp):
                par = [0, 0]

                def stageA(bt, hi):
                    h = 2 * hb + hi

                    def mds(e):
                        for k in range(4):
                            c = bt * 4 + k
                            ins = e.matmul(dsp[:, k * 128:(k + 1) * 128], lhsT=khtok[hi][:, c, :], rhs=vtok[hi][:, c, :],
                                           start=True, stop=True)
                        return ins
                    S.op("pe", mds, [f"khtok{hi}", f"vtok{hi}"], ["dsp"])
                    for k in range(4):
                        c = bt * 4 + k
                        seg_s = 0 if not sample else (1 + c // 2)
                        seg_start = (c == 0) or (sample and c == 2)
                        seg_end = (c == nch - 1) or (sample and c == 1)
                        p = par[hi]
                        cur, ck = S32x[hi][p], f"S32_{hi}_{p}"
                        nxt, nk_ = S32x[hi][1 - p], f"S32_{hi}_{1 - p}"
                        if seg_start:
                            if seg_s == 0:
                                if is_first_tile:
                                    MS("dve", cur[:], 0.0, [ck])
                                else:
                                    LD(cur[:], stp_d[l, h], [ck], chan="io")
                            else:
                                LD(cur[:], sth_d[l, seg_s - 1, h], [ck], chan="io")
                        CP("dve", S16q[hi][:, k, :], cur[:], [ck], [f"S16_{hi}"])
                        STT(nxt[:], cur[:], eend[pp][hi][:, c:c + 1], dsp[:, k * 128:(k + 1) * 128], ALU.mult, ALU.add,
                            [ck, f"eend{pp}{hi}", "dsp"], [nk_])
                        par[hi] = 1 - p
                        if seg_end:
                            dst = stp_d[l, h] if seg_s == 0 else sts_d[l, seg_s - 1, h]
                            ins = S.op("sp", lambda e, dst=dst, nxt=nxt: e.dma_start(out=dst, in_=nxt[:]), [nk_], (), dma="io")
                            if seg_s != 0 or is_last_prompt_tile:
                                out_dmas.append(ins)

                def stageB(bt, hi):
                    oslot = otp[:, hi * 128:(hi + 1) * 128]

                    def mo4(e):
                        for k in range(4):
                            c = bt * 4 + k
                            e.matmul(oslot[:, k * 32:(k + 1) * 32], lhsT=vtok[hi][:, c, :], rhs=AT16[hi][:, c * 32:(c + 1) * 32],
                                     start=True, stop=False)
                            ins = e.matmul(oslot[:, k * 32:(k + 1) * 32], lhsT=S16q[hi][:, k, :],
                                           rhs=Qt16[pp][hi][:, c * 32:(c + 1) * 32], start=False, stop=True)
                        return ins
                    S.op("pe", mo4, [f"vtok{hi}", f"AT16_{hi}", f"S16_{hi}", f"Qt16_{pp}{hi}"], [f"otp{hi}"])
                    CP("act", o32[hi][:, bt * 128:(bt + 1) * 128], oslot[:, 0:128], [f"otp{hi}"], [f"o32_{hi}"])

                prev = None
                for bt in range(nch // 4):
                    for hi in range(2):
                        if prev is not None:
                            stageB(*prev)
                        stageA(bt, hi)
                        prev = (bt, hi)
                        yield
                stageB(*prev)
                yield
                for hi in range(2):
                    h = 2 * hb + hi
                    ACT(sq16[hi][:, 0:T], o32[hi][:, 0:T], AF.Square, [f"o32_{hi}"], [f"sq16_{hi}"])
                    S.op("pe", lambda e, hi=hi: e.matmul(dnp[:, 0:T], lhsT=ones16[:], rhs=sq16[hi][:, 0:T], start=True, stop=True),
                         [f"sq16_{hi}", "ones16"], ["dnp"])
                    rs, rsk = next_scr()
                    ACT(rs[:, 0:T], dnp[:, 0:T], AF.Ln, ["dnp"], [rsk], scale=1.0 / 128, bias=EPS)
                    ACT(rs[:, 0:T], rs[:, 0:T], AF.Exp, [rsk], [rsk], scale=-0.5)
                    STT(rs[:, 0:T], o32[hi][:, 0:T], hg[:, l, h:h + 1], rs[:, 0:T], ALU.mult, ALU.mult,
                        [f"o32_{hi}", "hg", rsk], [rsk])
                    TT("dve", ua16[:, h, 0:T], rs[:, 0:T], gatea[pp][hi][:, 0:T], ALU.mult, [rsk, f"gatea{pp}{hi}"], [f"ua{h}"])
                    yield

            def attn_unit_g(g, c0, nq, kblocks, zero):
                for bi, (kc0, nk, vb, vblk) in enumerate(kblocks):
                    sps, spk = (atp, "atp") if bi == 0 else next_pj()

                    def ms(e, sps=sps, kc0=kc0, nk=nk):
                        for hl in range(4):
                            ins = e.matmul(sps[0:nk, hl * nq:(hl + 1) * nq], lhsT=kT[:, g, kc0:kc0 + nk],
                                           rhs=qT[:, 4 * g + hl, c0:c0 + nq], start=True, stop=True)
                        return ins
                    S.op("pe", ms, ["kT", "qT"], [spk])
                    ACT(PT[bi][0:nk, 0:4 * nq], sps[0:nk, 0:4 * nq], AF.Exp, [spk], [f"PT{bi}"], scale=0.125)
                    if zero:
                        pv = PT[bi][:, 0:512].rearrange("p (h q) -> p h q", q=128)
                        if vb == "prev":
                            MS("dve", pv[0:64, :, 64:128], 0.0, [f"PT{bi}"])
                        else:
                            MS("dve", pv[64:128, :, 0:64], 0.0, [f"PT{bi}"])
                yield

                def mpv(e):
                    for bi, (kc0, nk, vb, vblk) in enumerate(kblocks):
                        ins = e.matmul(dsp[:, 0:4 * nq], lhsT=vdup[0:nk, vblk, g, :], rhs=PT[bi][0:nk, 0:4 * nq],
                                       start=(bi == 0), stop=(bi == len(kblocks) - 1))
                    for bi, (kc0, nk, vb, vblk) in enumerate(kblocks):
                        ins = e.matmul(dnp[:, 0:4 * nq], lhsT=ones16[0:nk, :], rhs=PT[bi][0:nk, 0:4 * nq],
                                       start=(bi == 0), stop=(bi == len(kblocks) - 1))
                    return ins
                S.op("pe", mpv, [f"PT{bi}" for bi in range(len(kblocks))] + [f"vdup{kb[3]}" for kb in kblocks] + ["ones16"],
                     ["dsp", "dnp"])
                den, dk2 = next_scr()
                TT("dve", den[:, 0:4 * nq].rearrange("p (h q) -> p h q", q=nq),
                   dnp[:, 0:4 * nq].rearrange("p (h q) -> p h q", q=nq),
                   esink[:, l * 16 + 4 * g:l * 16 + 4 * g + 4].unsqueeze(2).to_broadcast([128, 4, nq]), ALU.add,
                   ["dnp", "esink"], [dk2])
                ACT(den[:, 0:4 * nq], den[:, 0:4 * nq], AF.Ln, [dk2], [dk2])
                ACT(den[:, 0:4 * nq], den[:, 0:4 * nq], AF.Exp, [dk2], [dk2], scale=-1.0)
                tmp, tk = next_scr()
                TT("dve", tmp[:, 0:4 * nq], dsp[:, 0:4 * nq], den[:, 0:4 * nq], ALU.mult, ["dsp", dk2], [tk])
                for half in range(2):
                    r0 = half * 64
                    tv = tmp[r0:r0 + 64, 0:4 * nq].rearrange("p (j x q) -> p j x q", x=2, q=nq)[:, :, half, :]
                    uv = ub16[r0:r0 + 64, 2 * g:2 * g + 2, c0:c0 + nq]
                    TT("dve", uv, tv, uv, ALU.mult, [tk, f"ub{2 * g}", f"ub{2 * g + 1}"], [f"ub{2 * g}", f"ub{2 * g + 1}"])
                yield

            def attn_gen(g):
                for qb in range(4):
                    kbl = []
                    if not (is_first_tile and qb == 0):
                        kbl.append((qb * 128, 128, "prev", qb))
                    kbl.append((128 + qb * 128, 128, "own", qb + 1))
                    yield from attn_unit_g(g, qb * 128, 128, kbl, True)

            def step(gen, n=1):
                if gen is None:
                    return None
                for _ in range(n):
                    try:
                        next(gen)
                    except StopIteration:
                        return None
                return gen

            def drain(gen):
                while gen is not None:
                    gen = step(gen)

            if not sample:
                core = None
                attn = []

                def step_attn(n):
                    for _ in range(n):
                        if attn:
                            attn[0] = step(attn[0], 1)
                            if attn[0] is None:
                                attn.pop(0)

                for hb in range(4):
                    pp = hb % 2
                    prel = None
                    for ci in fm_pair(hb, pp):
                        if ci < 9:
                            core = step(core, 1)
                            step_attn(1)
                        else:
                            if prel is None:
                                drain(core)
                                core = None
                                prel = prelude_gen(hb, pp)
                            prel = step(prel, 4)
                    drain(prel)
                    core = chunkloop_gen(hb, pp)
                    attn.append(attn_gen(hb))
                while core is not None or attn:
                    core = step(core, 1)
                    if attn:
                        attn[0] = step(attn[0], 1)
                        if attn[0] is None:
                            attn.pop(0)
                CP("dve", khist[:, l, :, :], kT[:, :, 512:640], ["kT"], ["khist"])
                CP("dve", vhist[:, l, :, :], vdup[:, 4, :, :], ["vdup4"], ["vhist"])
            else:
                for hb in range(4):
                    pp = hb % 2
                    for _ in fm_pair(hb, pp):
                        pass
                    drain(prelude_gen(hb, pp))
                    drain(chunkloop_gen(hb, pp))
                for si in range(2):
                    LD(kv32[:, 0:256], cwk_d[l, si], ["kv32"], chan="io")
                    LD(kv32[:, 256:512], cwv_d[l, si], ["kv32"], chan="io")
                    CP("dve", k16[0][:, :], kv32[:, 0:256], ["kv32"], ["k16_0"])
                    CP("dve", vdup[:, 0, :, :].rearrange("p g (r d) -> p g r d", r=2),
                       kv32[:, 256:512].rearrange("p (g d) -> p g d", d=64).unsqueeze(2).to_broadcast([128, 4, 2, 64]),
                       ["kv32"], ["vdup0"])

                    def trk2(e):
                        for g in range(4):
                            ins = e.transpose(trp[0:64, g * 128:(g + 1) * 128], k16[0][:, g * 64:(g + 1) * 64], ident16[:])
                        return ins
                    S.op("pe", trk2, ["k16_0", "ident16"], ["trp"])
                    CP("dve", kT[:, :, 0:128], trp[0:64, 0:512].rearrange("p (g t) -> p g t", t=128), ["trp"], ["kT"])
                    for g in range(4):
                        drain(attn_unit_g(g, si * 64, 64, [(0, 128, "hist", 0), (128 + si * 64, 64, "own", 1 + si)], False))
            if STOP == "attn":
                DBG("ub", ub16[:].rearrange("p h t -> p (h t)"), [128, 8 * 512], [f"ub{h}" for h in range(8)])
            stop("attn")
            first_merge = True
            for j in range(16):
                w, wk = ws.get()
                wma = w[:, 0:2048].rearrange("p (k n) -> p k n", n=128)
                wmb = w[:, 2048:4096].rearrange("p (k n) -> p k n", n=128)
                wpa = w[:, 4096:5120].rearrange("p (k n) -> p k n", n=128)
                wpb = w[:, 5120:6144].rearrange("p (k n) -> p k n", n=128)
                t1 = None
                for which, wg, wp, u, ukeys in (("a", wma, wpa, ua16, [f"ua{h}" for h in range(8)]),
                                                ("b", wmb, wpb, ub16, [f"ub{h}" for h in range(8)])):
                    pj, pk = next_pj()

                    def mg(e, wg=wg, pj=pj):
                        for kc in range(KC):
                            ins = e.matmul(pj[:, 0:T], lhsT=wg[:, kc, :], rhs=hT[:, kc, 0:T], start=(kc == 0), stop=(kc == KC - 1))
                        return ins
                    S.op("pe", mg, [wk, "hT"], [pk])
                    sg_, sgk = sgt16[0 if which == "a" else 1], f"sgt16_{0 if which == 'a' else 1}"
                    ACT(sg_[:, 0:T], pj[:, 0:T], AF.Sigmoid, [pk], [sgk])
                    pj2, pk2 = next_pj()

                    def mp(e, wp=wp, pj2=pj2, u=u):
                        for kc in range(8):
                            ins = e.matmul(pj2[:, 0:T], lhsT=wp[:, kc, :], rhs=u[:, kc, 0:T], start=(kc == 0), stop=(kc == 7))
                        return ins
                    S.op("pe", mp, [wk] + ukeys, [pk2])
                    if which == "a":
                        t1, t1k = next_scr()
                        TT("dve", t1[:, 0:T], pj2[:, 0:T], sg_[:, 0:T], ALU.mult, [pk2, sgk], [t1k])
                    else:
                        t2, t2k = next_scr()
                        TT("dve", t2[:, 0:T], pj2[:, 0:T], sg_[:, 0:T], ALU.mult, [pk2, sgk], [t2k])
                        wr = [f"mg{j}"] + (["big16", "qT"] if first_merge else [])
                        rd_ = [t1k, t2k] + ([] if first_merge else ["big16"])
                        TT("dve", mg16[:, j, 0:T], t1[:, 0:T], t2[:, 0:T], ALU.add, rd_, wr)
                        first_merge = False
                ws.done()

            if STOP == "merge":
                DBG("mg", big16[:], [128, KC * 512], [f"mg{j}" for j in range(16)])
            stop("merge")
            pend_sq = []
            for gq in range(4):
                w, wk = ws.get()
                wv = w[:, :].rearrange("p (k n) -> p k n", n=512)
                for i in range(4):
                    j = gq * 4 + i
                    pj, pk = next_pj()

                    def mo2(e, wv=wv, pj=pj, i=i):
                        for kc in range(KC):
                            ins = e.matmul(pj[:, 0:T], lhsT=wv[:, kc, i * 128:(i + 1) * 128], rhs=mg16[:, kc, 0:T],
                                           start=(kc == 0), stop=(kc == KC - 1))
                        return ins
                    S.op("pe", mo2, [wk] + [f"mg{jj}" for jj in range(16)] + ["big16"], [pk])
                    for (c0, n, s) in segs:
                        STT(xT[:, j, c0:c0 + n], pj[:, c0:c0 + n], modT[:, l, 32 + j, s:s + 1], xT[:, j, c0:c0 + n],
                            ALU.mult, ALU.add, [pk, "modT", "xT"], ["xT"])
                    if pend_sq:
                        pend_sq.pop(0)()
                    sq, sk = sq16[j % 2], f"sq16_{j % 2}"
                    ACT(sq[:, 0:T], xT[:, j, 0:T], AF.Square, ["xT"], [sk])
                    pend_sq.append(lambda sq=sq, sk=sk, j=j: S.op(
                        "pe", lambda e: e.matmul(dnp[:, 0:T], lhsT=ones16[:], rhs=sq[:, 0:T], start=(j == 0), stop=(j == 15)),
                        [sk, "ones16"], ["dnp"]))
                ws.done()
            while pend_sq:
                pend_sq.pop(0)()
            S.op("dve", lambda e: e.memset(qT[0:1, 0, 0:1], 0.0), (), ["big16", "qT"] + [f"mg{jj}" for jj in range(16)])

        tok0 = 0
        for (ti, T) in tiles:
            if STOP == "mod":
                break
            LD(xT[:, :, 0:T], xT_d[:, :, tok0:tok0 + T].rearrange("k p t -> p k t"), ["xT"])
            if T == 512:
                LD(rope[:, 0:4, :], rope_d[:, ti * 4:(ti + 1) * 4, :], ["rope"])
            else:
                LD(rope[:, 0:1, :], rope_d[:, NTB - 1:NTB, :], ["rope"])
            try:
                for l in range(DEPTH):
                    layer_tile(ti, T, l, is_first_tile=(ti == 0), is_last_prompt_tile=(ti == NPT - 1))
            except _Stop:
                break
            rstd, rk = rms_rstd(lambda kc: xT[:, kc, 0:T], KC, T, ["xT"], 1.0 / D, pre=True)
            for kc in range(KC):
                y, yk = next_scr()
                STT(y[:, 0:T], xT[:, kc, 0:T], fng[:, kc:kc + 1], rstd[:, 0:T], ALU.mult, ALU.mult, ["xT", "fng", rk], [yk])
                ST(yT_d[kc, :, tok0:tok0 + T], y[:, 0:T], [yk])
            tok0 += T

        S.wait_all("sp", out_dmas)
        S.finalize()
        with nc.Block() as block:
            @block.tensor
            def _(e):
                S.replay("pe", e, sems)

            @block.scalar
            def _(e):
                S.replay("act", e, sems)

            @block.vector
            def _(e):
                S.replay("dve", e, sems)

            @block.gpsimd
            def _(e):
                S.replay("pool", e, sems)

            @block.sync
            def _(e):
                S.replay("sp", e, sems)
    return nc


def _grp(wcols):
    K, n = wcols.shape
    return np.ascontiguousarray(wcols.reshape(K // 128, 128, n).transpose(1, 0, 2)).reshape(128, -1)


def _pad(a):
    out = np.zeros((128, GW), np.float32)
    out[:, :a.shape[1]] = a
    return out


def _weight_stream(w_in, w_pa, w_pb, w_out, DEPTH):
    groups = []
    for l in range(DEPTH):
        wi = w_in[l]
        c = lambda name, i, n=128: wi[:, OFF[name] + i * n: OFF[name] + (i + 1) * n]
        groups.append(_grp(wi[:, OFF["qb"]:OFF["qb"] + 512]))
        groups.append(_grp(wi[:, OFF["qb"] + 512:OFF["qb"] + 1024]))
        groups.append(_grp(wi[:, OFF["kb"]:OFF["kb"] + 512]))
        for hb in range(4):
            h0, h1 = 2 * hb, 2 * hb + 1
            groups.append(_grp(np.concatenate([c("fa", h0), c("fa", h1), c("ia", h0), c("ia", h1)], axis=1)))
            groups.append(_grp(np.concatenate([c("qa", h0), c("qa", h1), c("za", h0), c("za", h1)], axis=1)))
            groups.append(_grp(np.concatenate([c("zb", h0), c("zb", h1), c("ga", h0), c("ga", h1)], axis=1)))
        for j in range(16):
            a = np.concatenate([_grp(c("ma", j)), _grp(c("mb", j)),
                                _grp(w_pa[l][:, j * 128:(j + 1) * 128]), _grp(w_pb[l][:, j * 128:(j + 1) * 128])], axis=1)
            groups.append(_pad(a))
        for gq in range(4):
            groups.append(_grp(w_out[l][:, gq * 512:(gq + 1) * 512]))
    return np.stack(groups)


def _rope_table(SEQ):
    NTB = SEQ // 128 + 1
    inv = (500000.0 ** (-np.arange(0, 16, 2) / 16)).astype(np.float32)
    tab = np.zeros((128, NTB, 32), np.float32)
    for tb in range(NTB):
        if tb < NTB - 1:
            pos = (tb * 128 + np.arange(128)).astype(np.float32)
        else:
            pos = (PAST + (np.arange(128) % 64)).astype(np.float32)
        ang = pos[:, None] * inv[None, :]
        cs, sn = np.cos(ang).astype(np.float32), np.sin(ang).astype(np.float32)
        tab[:, tb, 0:8] = cs
        tab[:, tb, 8:16] = cs
        tab[:, tb, 16:24] = -sn
        tab[:, tb, 24:32] = sn
    return tab


_NC_CACHE = {}


def kernel(x_prompt, x_sample, c_prompt, c_sample, state_hgrn, cache_win_k, cache_win_v,
           ada_w, ada_b, norm_g, w_in, lb_logits, hgrn_norm_g, sinks, w_branch_a, w_branch_b, w_out,
           final_norm_g):
    f = lambda a: np.asarray(a, dtype=np.float32)
    x_prompt, x_sample, c_prompt, c_sample = f(x_prompt), f(x_sample), f(c_prompt), f(c_sample)
    state_hgrn, cache_win_k, cache_win_v = f(state_hgrn), f(cache_win_k), f(cache_win_v)
    ada_w, ada_b, norm_g, w_in, lb_logits = f(ada_w), f(ada_b), f(norm_g), f(w_in), f(lb_logits)
    hgrn_norm_g, sinks, w_pa, w_pb, w_out, final_norm_g = f(hgrn_norm_g), f(sinks), f(w_branch_a), f(w_branch_b), f(w_out), f(final_norm_g)
    NCORES = x_prompt.shape[0]
    SEQ = x_prompt.shape[1]
    DEPTH = w_in.shape[0]
    NBUF = CFG["NBUF"]
    key = (SEQ, DEPTH, NBUF, STOP, SKIP)
    if key not in _NC_CACHE:
        _NC_CACHE[key] = build_nc(SEQ, DEPTH, NBUF)
    nc = _NC_CACHE[key]

    adaw = np.stack([_grp(ada_w[l][:, g * 512:(g + 1) * 512]) for l in range(DEPTH) for g in range(12)])
    wst = _weight_stream(w_in, w_pa, w_pb, w_out, DEPTH)
    adab = np.ascontiguousarray(ada_b.reshape(DEPTH, 48, 128).transpose(2, 0, 1))
    normg = np.ascontiguousarray(norm_g.reshape(DEPTH, KC, 128).transpose(2, 0, 1))
    lbl4 = np.zeros((4, 1024), np.float32) - 1e4
    lbl4[:DEPTH] = lb_logits
    lbl = np.ascontiguousarray(lbl4.reshape(4, 8, 128).transpose(2, 1, 0))
    hgl = np.ascontiguousarray(hgrn_norm_g.reshape(DEPTH, 8, 128).transpose(2, 0, 1))
    fng = np.ascontiguousarray(final_norm_g.reshape(KC, 128).T)
    rope = _rope_table(SEQ)
    ident = np.eye(128, dtype=np.float32).astype(ml_dtypes.bfloat16)
    m32 = (np.arange(32)[:, None] <= np.arange(32)[None, :]).astype(np.float32)
    mask32 = m32
    scanm = np.ones((128, 512), np.float32)
    scanm[:, ::32] = 0.0
    mrow = np.zeros((1, 4, 512), np.float32)
    mrow[0, 0, 0:64] = 1.0
    mrow[0, 1, 64:128] = 1.0
    qv = mrow[0, 2].reshape(4, 128)
    qv[:, 64:128] = -1000.0
    qv = mrow[0, 3].reshape(4, 128)
    qv[:, 0:64] = -1000.0
    shared = dict(adaw=adaw, wst=wst, adab=adab, normg=normg, lbl=lbl, hg=hgl, sinks=np.ascontiguousarray(sinks.reshape(-1)),
                  fng=fng, rope=rope, ident=ident, mask32=mask32.astype(ml_dtypes.bfloat16),
                  scanm=scanm.astype(ml_dtypes.bfloat16))
    in_maps = []
    for b in range(NCORES):
        xs = np.concatenate([x_prompt[b], x_sample[2 * b], x_sample[2 * b + 1]], axis=0)
        xT = np.ascontiguousarray(xs.T.reshape(KC, 128, -1))
        cs = np.stack([c_prompt[b], c_sample[2 * b], c_sample[2 * b + 1]], axis=1)
        cT = np.ascontiguousarray(cs.reshape(KC, 128, 3).transpose(1, 0, 2))
        m = dict(shared)
        m.update(xT=xT, cT=cT,
                 sth=np.ascontiguousarray(state_hgrn[:, 2 * b:2 * b + 2]),
                 cwk=np.ascontiguousarray(cache_win_k[:, 2 * b:2 * b + 2].reshape(DEPTH, 2, 128, 256)),
                 cwv=np.ascontiguousarray(cache_win_v[:, 2 * b:2 * b + 2].reshape(DEPTH, 2, 128, 256)))
        in_maps.append(m)
    res = run_bass_kernel_spmd(nc, in_maps, core_ids=list(range(NCORES)))
    R = res.results
    LAST["R"] = R
    y_prompt = np.stack([R[b]["yT"][:, :, :SEQ].reshape(D, SEQ).T for b in range(NCORES)])
    y_sample = np.stack([R[b]["yT"][:, :, SEQ + 64 * s:SEQ + 64 * (s + 1)].reshape(D, 64).T
                         for b in range(NCORES) for s in range(2)])
    stp = np.stack([R[b]["stp"] for b in range(NCORES)], axis=1)
    wkp = np.stack([R[b]["wkp"].reshape(DEPTH, 128, 4, 64) for b in range(NCORES)], axis=1)
    wvp = np.stack([R[b]["wvp"].reshape(DEPTH, 128, 4, 64) for b in range(NCORES)], axis=1)
    sts = np.concatenate([R[b]["sts"] for b in range(NCORES)], axis=1)
    nks = np.concatenate([R[b]["nks"].reshape(DEPTH, 2, 64, 4, 64) for b in range(NCORES)], axis=1)
    nvs = np.concatenate([R[b]["nvs"].reshape(DEPTH, 2, 64, 4, 64) for b in range(NCORES)], axis=1)
    o = lambda a: np.ascontiguousarray(a, dtype=np.float32)
    return (o(y_prompt), o(y_sample), o(stp), o(wkp), o(wvp), o(sts), o(nks), o(nvs))
```

```python
import contextlib
import numpy as np
import ml_dtypes
import concourse.bass as bass
import concourse.mybir as mybir
from concourse.bass_utils import run_bass_kernel_spmd

F32 = mybir.dt.float32
BF16 = mybir.dt.bfloat16
AF = mybir.ActivationFunctionType
ALU = mybir.AluOpType
AX = mybir.AxisListType

D = 2048
KC = 16
NIN = 11776
OFF = dict(qa=0, fa=1024, ia=2048, ga=3072, za=4096, qb=5120, kb=6144, vb=6400, zb=6656, ma=7680, mb=9728)
PAST = 4096
EPS = 1e-6
CFG = dict(SEQ=4096, DEPTH=4, NCORES=8, NBUF=2)
NGRP = 35
GW = 8192
import os
STOP = os.environ.get("KSTOP", "")
SKIP = os.environ.get("KSKIP", "")
LAST = {}


class _Stop(Exception):
    pass


class _Instr:
    __slots__ = ("chan", "idx", "needs_inc", "value", "fn", "waits", "eng", "is_dma")

    def __init__(self, chan, idx, fn, eng, is_dma):
        self.chan, self.idx, self.fn, self.eng, self.is_dma = chan, idx, fn, eng, is_dma
        self.needs_inc = is_dma
        self.value = None
        self.waits = []


class Sched:
    ENGS = ("pe", "act", "dve", "pool", "sp")

    def __init__(self):
        self.lists = {e: [] for e in self.ENGS}
        self.chan_n, self.chan_last = {}, {}
        self.seen = {e: {} for e in self.ENGS}
        self.lw, self.rd = {}, {}
        self.dma_chans = []

    def _dep(self, eng, ins, p):
        if p is None or p is ins:
            return
        if p.chan == eng and eng == "pe":
            return
        s = self.seen[eng]
        if s.get(p.chan, -1) >= p.idx:
            return
        s[p.chan] = p.idx
        p.needs_inc = True
        ins.waits.append(p)

    PSUM_BANK = {"pj0": "pj0", "pj1": "pj1", "pj2": "pj2", "trp": "trp", "atp": "atp", "otp": "otp", "otp0": "otp",
                 "otp1": "otp", "dsp": "dsp", "dsp0": "dsp", "dsp1": "dsp", "dsp2": "dsp", "dsp3": "dsp", "dnp": "dnp"}

    def op(self, eng, fn, reads=(), writes=(), dma=None):
        pb = self.PSUM_BANK
        banks = [pb[k] for k in list(reads) + list(writes) if k in pb]
        reads = [k for k in reads if k not in pb]
        writes = [k for k in writes if k not in pb] + sorted(set(banks))
        chan = dma if dma is not None else eng
        if dma is not None and chan not in self.chan_n:
            self.dma_chans.append(chan)
        idx = self.chan_n.get(chan, 0)
        self.chan_n[chan] = idx + 1
        ins = _Instr(chan, idx, fn, eng, dma is not None)
        if dma is not None:
            self._dep(eng, ins, self.chan_last.get(chan))
        self.chan_last[chan] = ins
        for k in reads:
            self._dep(eng, ins, self.lw.get(k))
        for k in writes:
            self._dep(eng, ins, self.lw.get(k))
            for r in self.rd.get(k, ()):
                self._dep(eng, ins, r)
        for k in writes:
            self.lw[k] = ins
            self.rd[k] = []
        for k in reads:
            if k not in writes:
                self.rd.setdefault(k, []).append(ins)
        self.lists[eng].append(ins)
        return ins

    def wait_all(self, eng, instrs):
        ins = _Instr(eng, -1, None, eng, False)
        for p in instrs:
            self._dep(eng, ins, p)
        self.lists[eng].append(ins)

    def finalize(self):
        for eng in self.ENGS:
            c = 0
            for ins in self.lists[eng]:
                if ins.fn is None:
                    continue
                if ins.is_dma:
                    ins.value = 16 * (ins.idx + 1)
                elif ins.needs_inc:
                    c += 1
                    ins.value = c

    def replay(self, eng, handle, sems):
        for ins in self.lists[eng]:
            for p in ins.waits:
                handle.wait_ge(sems[p.chan], p.value)
            if ins.fn is None:
                continue
            r = ins.fn(handle)
            if ins.is_dma:
                r.then_inc(sems[ins.chan], 16)
            elif ins.needs_inc:
                r.then_inc(sems[ins.chan], 1)


def build_nc(SEQ, DEPTH, NBUF):
    NPT = SEQ // 512
    NTOK = SEQ + 128
    NTB = SEQ // 128 + 1
    nc = bass.Bass("TRN2", target_bir_lowering=False)
    S = Sched()

    def din(name, shape, dt=F32):
        return nc.dram_tensor(name, list(shape), dt, kind="ExternalInput").ap()

    def dout(name, shape):
        return nc.dram_tensor(name, list(shape), F32, kind="ExternalOutput").ap()

    xT_d = din("xT", [KC, 128, NTOK])
    cT_d = din("cT", [128, KC, 3])
    adaw_d = din("adaw", [DEPTH * 12, 128, GW])
    adab_d = din("adab", [128, DEPTH, 48])
    normg_d = din("normg", [128, DEPTH, KC])
    lbl_d = din("lbl", [128, 8, 4])
    hg_d = din("hg", [128, DEPTH, 8])
    sinks_d = din("sinks", [DEPTH * 16])
    fng_d = din("fng", [128, KC])
    rope_d = din("rope", [128, NTB, 32])
    ident_d = din("ident", [128, 128], BF16)
    mask32_d = din("mask32", [32, 32], BF16)
    scanm_d = din("scanm", [128, 512], BF16)
    wst_d = din("wst", [DEPTH * NGRP, 128, GW])
    sth_d = din("sth", [DEPTH, 2, 8, 128, 128])
    cwk_d = din("cwk", [DEPTH, 2, 128, 256])
    cwv_d = din("cwv", [DEPTH, 2, 128, 256])

    yT_d = dout("yT", [KC, 128, NTOK])
    stp_d = dout("stp", [DEPTH, 8, 128, 128])
    wkp_d = dout("wkp", [DEPTH, 128, 256])
    wvp_d = dout("wvp", [DEPTH, 128, 256])
    sts_d = dout("sts", [DEPTH, 2, 8, 128, 128])
    nks_d = dout("nks", [DEPTH, 2, 64, 256])
    nvs_d = dout("nvs", [DEPTH, 2, 64, 256])

    es = contextlib.ExitStack()
    with es:
        def sb(name, shape, dt=F32):
            return es.enter_context(nc.sbuf_tensor(name, list(shape), dt))

        def ps(name, shape, dt=F32):
            return es.enter_context(nc.psum_tensor(name, list(shape), dt))

        wbuf = [sb(f"wbuf{i}", [128, GW], BF16) for i in range(NBUF)]
        xT = sb("xTs", [128, KC, 512])
        hT = sb("hT", [128, KC, 512], BF16)
        big16 = sb("big16", [128, KC * 512], BF16)
        qT = big16[0:64, :].rearrange("p (h t) -> p h t", t=512)
        mg16 = big16[:, :].rearrange("p (k t) -> p k t", t=512)
        NSCR = 3
        scr = [sb(f"scr{i}", [128, 512]) for i in range(NSCR)]
        qtok = [sb(f"qtok{i}", [128, 512], BF16) for i in range(2)]
        kv32 = sb("kv32", [128, 512])
        k16 = [sb(f"k16_{i}", [128, 256], BF16) for i in range(2)]
        vdup = sb("vdup", [128, 5, 4, 128], BF16)
        kT = sb("kT", [64, 4, 640], BF16)
        khist = sb("khist", [64, DEPTH, 4, 128], BF16)
        vhist = sb("vhist", [128, DEPTH, 4, 128], BF16)
        ua16 = sb("ua16", [128, 8, 512], BF16)
        ub16 = sb("ub16", [128, 8, 512], BF16)
        gatea = [[sb(f"gatea{p}{i}", [128, 512], BF16) for i in range(2)] for p in range(2)]
        Qt16 = [[sb(f"Qt16_{p}{i}", [128, 512], BF16) for i in range(2)] for p in range(2)]
        Kt16 = [sb(f"Kt16_{i}", [128, 512], BF16) for i in range(2)]
        kh16 = [sb(f"kh16_{i}", [128, 512], BF16) for i in range(2)]
        Va16 = [sb(f"Va16_{i}", [128, 512], BF16) for i in range(2)]
        khtok = [sb(f"khtok{i}", [32, 16, 128], BF16) for i in range(2)]
        vtok = [sb(f"vtok{i}", [32, 16, 128], BF16) for i in range(2)]
        AT16 = [sb(f"AT16_{i}", [32, 512], BF16) for i in range(2)]
        eend = [[sb(f"eend{p}{i}", [128, 16]) for i in range(2)] for p in range(2)]
        o32 = [sb(f"o32_{i}", [128, 512]) for i in range(2)]
        S32x = [[sb(f"S32_{i}_{p}", [128, 128]) for p in range(2)] for i in range(2)]
        S16q = [sb(f"S16q_{i}", [128, 4, 128], BF16) for i in range(2)]
        PT = [sb(f"PT{i}", [128, 512], BF16) for i in range(2)]
        sq16 = [sb(f"sq16_{i}", [128, 512], BF16) for i in range(2)]
        sgt16 = [sb(f"sgt16_{i}", [128, 512], BF16) for i in range(2)]
        faS = [[sb(f"faS{i}{k}", [128, 512]) for k in range(3)] for i in range(2)]
        rstdS = sb("rstdS", [128, 512])
        ident16 = sb("ident16", [128, 128], BF16)
        ones16 = sb("ones16", [128, 128], BF16)
        mask32 = sb("mask32s", [32, 32], BF16)
        scanm = sb("scanms", [128, 512], BF16)
        rope = sb("ropes", [128, 4, 32])
        cT16 = sb("cT16", [128, KC, 3], BF16)
        modT = sb("modT", [128, DEPTH, 48, 3])
        Gm = sb("Gm", [128, DEPTH, KC, 3])
        fng = sb("fngs", [128, KC])
        lbv = sb("lbv", [128, 8, 4])
        lb1 = sb("lb1", [128, 8, 4])
        nlb1 = sb("nlb1", [128, 8, 4])
        hg = sb("hgs", [128, DEPTH, 8])
        esink = sb("esink", [128, DEPTH * 16])

        _o = o32[0]
        adab = _o[:, 0:DEPTH * 48].rearrange("p (l c) -> p l c", c=48)
        normg = _o[:, 192:192 + DEPTH * KC].rearrange("p (l c) -> p l c", c=KC)
        lbl = _o[:, 256:288].rearrange("p (h l) -> p h l", l=4)
        lbe = _o[:, 288:320].rearrange("p (h l) -> p h l", l=4)
        lbs = _o[:, 320:328]
        cT32 = _o[:, 328:376].rearrange("p (k s) -> p k s", s=3)
        PJ = [ps(f"pj{i}", [128, 512]) for i in range(3)]
        trp = ps("trp", [128, 1024], BF16)
        atp = ps("atp", [128, 512])
        otp = ps("otp", [128, 512])
        dsp = ps("dsp", [128, 512])
        dnp = ps("dnp", [128, 512])

        sem_names = list(S.ENGS) + ["ld", "st", "st2", "io"] + [f"w{i}" for i in range(NBUF)]
        sems = {n: es.enter_context(nc.semaphore(n)) for n in sem_names}

        cnt = {"pj": 0, "scr": 0, "st": 0}

        def next_pj():
            i = cnt["pj"] % 3
            cnt["pj"] += 1
            return PJ[i], f"pj{i}"

        def next_scr():
            i = cnt["scr"] % NSCR
            cnt["scr"] += 1
            return scr[i], f"scr{i}"

        def A(eng, out, in_, func, reads, writes, **kw):
            S.op(eng, lambda e: e.activation(out=out, in_=in_, func=func, **kw), reads, writes)

        def ACT(out, in_, func, reads, writes, **kw):
            A("act", out, in_, func, reads, writes, **kw)

        def TT(eng, out, in0, in1, op, reads, writes):
            S.op(eng, lambda e: e.tensor_tensor(out=out, in0=in0, in1=in1, op=op), reads, writes)

        def TS(eng, out, in0, s1, s2, op0, op1, reads, writes):
            S.op(eng, lambda e: e.tensor_scalar(out=out, in0=in0, scalar1=s1, scalar2=s2, op0=op0, op1=op1), reads, writes)

        def STT(out, in0, scalar, in1, op0, op1, reads, writes):
            S.op("dve", lambda e: e.scalar_tensor_tensor(out=out, in0=in0, scalar=scalar, in1=in1, op0=op0, op1=op1), reads, writes)

        def CP(eng, out, in_, reads, writes):
            if eng == "act":
                ACT(out, in_, AF.Copy, reads, writes)
            else:
                S.op(eng, lambda e: e.tensor_copy(out=out, in_=in_), reads, writes)

        def MS(eng, ap, val, writes):
            S.op(eng, lambda e: e.memset(ap, val), (), writes)

        def LD(out, in_, writes, chan="ld"):
            return S.op("sp", lambda e: e.dma_start(out=out, in_=in_), (), writes, dma=chan)

        out_dmas = []

        def ST(out, in_, reads, extra_writes=()):
            chan = ("st", "st2")[cnt["st"] % 2]
            cnt["st"] += 1
            ins = S.op("sp", lambda e: e.dma_start(out=out, in_=in_), reads, extra_writes, dma=chan)
            out_dmas.append(ins)
            return ins

        def DBG(name, ap, shape, reads):
            d = nc.dram_tensor("dbg_" + name, list(shape), ap.dtype, kind="ExternalOutput").ap()
            ST(d, ap, reads)

        def stop(phase):
            if STOP == phase:
                raise _Stop()

        class WS:
            def __init__(self):
                self.plan = []
                self.nissued = 0
                self.nget = 0

            def add(self, ap, ncols=GW):
                self.plan.append((ap, ncols))

            def _issue(self):
                if self.nissued >= len(self.plan):
                    return
                i = self.nissued
                b = i % NBUF
                ap, ncols = self.plan[i]
                S.op("pool", lambda e: e.dma_start(out=wbuf[b][:, 0:ncols], in_=ap[:, 0:ncols]),
                     (), [f"wbuf{b}"], dma=f"w{b}")
                self.nissued += 1

            def start(self):
                for _ in range(NBUF):
                    self._issue()

            def get(self):
                b = self.nget % NBUF
                assert self.nget < self.nissued
                self.nget += 1
                return wbuf[b], f"wbuf{b}"

            def done(self):
                self._issue()

        ws = WS()
        for g in range(DEPTH * 12):
            ws.add(adaw_d[g])
        tiles = [(t, 512) for t in range(NPT)] + [(NPT, 128)]
        for (ti, T) in tiles:
            for l in range(DEPTH):
                for g in range(NGRP):
                    ncols = 6144 if 15 <= g < 31 else GW
                    ws.add(wst_d[l * NGRP + g], ncols)

        LD(ident16[:], ident_d, ["ident16"])
        LD(mask32[:], mask32_d, ["mask32"])
        LD(scanm[:], scanm_d, ["scanm"])
        LD(cT32[:], cT_d, ["cT32"])
        LD(adab[:], adab_d, ["adab"])
        LD(normg[:], normg_d, ["normg"])
        LD(fng[:], fng_d, ["fng"])
        LD(lbl[:], lbl_d, ["lbl"])
        LD(hg[:], hg_d, ["hg"])
        LD(esink[:], sinks_d.partition_broadcast(128), ["esink"])
        ws.start()
        MS("dve", ones16[:], 1.0, ["ones16"])
        CP("dve", cT16[:], cT32[:], ["cT32"], ["cT16"])
        ACT(esink[:], esink[:], AF.Exp, ["esink"], ["esink"])
        ACT(lbe[:], lbl[:], AF.Exp, ["lbl"], ["lbe"])
        S.op("dve", lambda e: e.tensor_reduce(out=lbs[:], in_=lbe[:], axis=AX.X, op=ALU.add), ["lbe"], ["lbs"])
        S.op("dve", lambda e: e.reciprocal(out=lbs[:], in_=lbs[:]), ["lbs"], ["lbs"])
        TT("dve", lbe[:], lbe[:], lbs[:].unsqueeze(2).to_broadcast([128, 8, 4]), ALU.mult, ["lbe", "lbs"], ["lbe"])
        MS("dve", lbv[:], 0.0, ["lbv"])
        for l in range(1, 4):
            TT("dve", lbv[:, :, l:l + 1], lbv[:, :, l - 1:l], lbe[:, :, l:l + 1], ALU.add, ["lbv", "lbe"], ["lbv"])
        TS("dve", lb1[:], lbv[:], -1.0, 1.0, ALU.mult, ALU.add, ["lbv"], ["lb1"])
        TS("dve", nlb1[:], lbv[:], 1.0, -1.0, ALU.mult, ALU.add, ["lbv"], ["nlb1"])

        for l in range(DEPTH):
            for g in range(12):
                w, wk = ws.get()
                wv = w[:, :].rearrange("p (k n) -> p k n", n=512)
                pj, pk = next_pj()

                def mm(e, wv=wv, pj=pj):
                    for i in range(4):
                        for kc in range(KC):
                            ins = e.matmul(pj[:, i * 4:i * 4 + 3], lhsT=wv[:, kc, i * 128:(i + 1) * 128],
                                           rhs=cT16[:, kc, :], start=(kc == 0), stop=(kc == KC - 1))
                    return ins
                S.op("pe", mm, [wk, "cT16"], [pk])
                ws.done()
                TT("dve", modT[:, l, g * 4:(g + 1) * 4, :], pj[:, 0:16].rearrange("p (i s) -> p i s", s=4)[:, :, 0:3],
                   adab[:, l, g * 4:(g + 1) * 4].unsqueeze(2).to_broadcast([128, 4, 3]), ALU.add,
                   [pk, "adab"], ["modT"])
        for l in range(DEPTH):
            TS("dve", Gm[:, l, :, :], modT[:, l, 16:32, :], 1.0, None, ALU.add, ALU.bypass, ["modT"], ["Gm"])
            TT("dve", Gm[:, l, :, :], Gm[:, l, :, :], normg[:, l, :].unsqueeze(2).to_broadcast([128, KC, 3]),
               ALU.mult, ["Gm", "normg"], ["Gm"])

        if STOP == "mod":
            DBG("modT", modT[:].rearrange("p l c s -> p (l c s)"), [128, DEPTH * 48 * 3], ["modT"])
            DBG("Gm", Gm[:].rearrange("p l c s -> p (l c s)"), [128, DEPTH * KC * 3], ["Gm"])
            DBG("lbv", lbv[:].rearrange("p h l -> p (h l)"), [128, 32], ["lbv"])
            DBG("esink", esink[:], [128, DEPTH * 16], ["esink"])
        def rms_rstd(src_fn, nchunk, T, key_reads, scale, pre=False):
            for kc in range(0 if pre else nchunk):
                sq, sk = sq16[kc % 2], f"sq16_{kc % 2}"
                ACT(sq[:, 0:T], src_fn(kc), AF.Square, key_reads, [sk])
                S.op("pe", lambda e, sq=sq, kc=kc: e.matmul(dnp[:, 0:T], lhsT=ones16[:], rhs=sq[:, 0:T],
                                                           start=(kc == 0), stop=(kc == nchunk - 1)),
                     [sk, "ones16"], ["dnp"])
            ln, lk = rstdS, "rstdS"
            ACT(ln[:, 0:T], dnp[:, 0:T], AF.Ln, ["dnp"], [lk], scale=scale, bias=EPS)
            ACT(ln[:, 0:T], ln[:, 0:T], AF.Exp, [lk], [lk], scale=-0.5)
            return ln, lk

        def segs_of(T):
            return [(0, 512, 0)] if T == 512 else [(0, 64, 1), (64, 64, 2)]

        def layer_tile(ti, T, l, is_first_tile, is_last_prompt_tile):
            sample = (T == 128)
            segs = segs_of(T)
            blocks = [(i * 128, 128, 0) for i in range(4)] if not sample else [(0, 64, 1), (64, 64, 2)]
            nch = T // 32
            rstd, rk = rms_rstd(lambda kc: xT[:, kc, 0:T], KC, T, ["xT"], 1.0 / D, pre=(l > 0))
            for kc in range(KC):
                tmp, tk = next_scr()
                for (c0, n, s) in segs:
                    STT(tmp[:, c0:c0 + n], xT[:, kc, c0:c0 + n], Gm[:, l, kc, s:s + 1], rstd[:, c0:c0 + n],
                        ALU.mult, ALU.mult, ["xT", "Gm", rk], [tk])
                    ACT(hT[:, kc, c0:c0 + n], tmp[:, c0:c0 + n], AF.Identity, [tk, "modT"], ["hT"],
                        bias=modT[:, l, kc, s:s + 1])

            if STOP == "norm":
                DBG("hT", hT[:].rearrange("p k t -> p (k t)"), [128, KC * 512], ["hT"])
            stop("norm")
            def rope_fix(pj, pk, n, nh, slot, out_view, out_key):
                if "r" in SKIP:
                    return
                pv = pj[0:n, 0:nh * 64].rearrange("p (h d) -> p h d", d=64)
                t1, k1 = next_scr()
                t2, k2 = next_scr()
                t1v = t1[0:n, 0:nh * 16].rearrange("p (h d) -> p h d", d=16)
                t2v = t2[0:n, 0:nh * 16].rearrange("p (h d) -> p h d", d=16)
                TT("dve", t1v, pv[:, :, 0:16], rope[0:n, slot, 0:16].unsqueeze(1).to_broadcast([n, nh, 16]),
                   ALU.mult, [pk, "rope"], [k1])
                TT("dve", t2v[:, :, 0:8], pv[:, :, 8:16], rope[0:n, slot, 16:24].unsqueeze(1).to_broadcast([n, nh, 8]),
                   ALU.mult, [pk, "rope"], [k2])
                TT("dve", t2v[:, :, 8:16], pv[:, :, 0:8], rope[0:n, slot, 24:32].unsqueeze(1).to_broadcast([n, nh, 8]),
                   ALU.mult, [pk, "rope"], [k2])
                TT("dve", out_view, t1v, t2v, ALU.add, [k1, k2], [out_key])

            if not sample and not is_first_tile:
                CP("dve", kT[:, :, 0:128], khist[:, l, :, :], ["khist"], ["kT"])
                CP("dve", vdup[:, 0, :, :], vhist[:, l, :, :], ["vhist"], ["vdup0"])
            pend_tr = []
            for gi in range(3):
                w, wk = ws.get()
                wv = w[:, :].rearrange("p (k n) -> p k n", n=512)
                for bi, (c0, n, s) in enumerate(blocks):
                    slot = bi if not sample else 0
                    pj, pk = next_pj()

                    def mm(e, wv=wv, pj=pj, c0=c0, n=n):
                        for kc in range(KC):
                            ins = e.matmul(pj[0:n, :], lhsT=hT[:, kc, c0:c0 + n], rhs=wv[:, kc, :],
                                           start=(kc == 0), stop=(kc == KC - 1))
                        return ins
                    S.op("pe", mm, [wk, "hT"], [pk])
                    while pend_tr:
                        pend_tr.pop(0)()
                    rb = bi % 2
                    if gi < 2:
                        qt_, qk_ = qtok[rb], f"qtok{rb}"
                        ACT(qt_[0:n, 0:512], pj[0:n, :], AF.Copy, [pk], [qk_])
                        rope_fix(pj, pk, n, 8, slot,
                                 qt_[0:n, 0:512].rearrange("p (h d) -> p h d", d=64)[:, :, 0:16], qk_)

                        def do_trq(qt_=qt_, qk_=qk_, n=n, c0=c0, gi=gi, bi=bi):
                            def trq(e):
                                for hh in range(8):
                                    ins = e.transpose(trp[0:64, hh * 128:hh * 128 + n], qt_[0:n, hh * 64:(hh + 1) * 64],
                                                      ident16[0:n, 0:n])
                                return ins
                            S.op("pe", trq, [qk_, "ident16"], ["trp"])
                            CP("dve" if bi % 2 == 0 else "act", qT[:, gi * 8:(gi + 1) * 8, c0:c0 + n],
                               trp[0:64, :].rearrange("p (h t) -> p h t", t=128)[:, :, 0:n], ["trp", "big16"], ["qT"])
                        pend_tr.append(do_trq)
                    else:
                        k16_, k16k = k16[rb], f"k16_{rb}"
                        ACT(kv32[0:n, :], pj[0:n, :], AF.Copy, [pk], ["kv32"])
                        rope_fix(pj, pk, n, 4, slot,
                                 kv32[0:n, 0:256].rearrange("p (h d) -> p h d", d=64)[:, :, 0:16], "kv32")
                        CP("dve", k16_[0:n, :], kv32[0:n, 0:256], ["kv32"], [k16k])
                        CP("dve", vdup[0:n, 1 + bi, :, :].rearrange("p g (r d) -> p g r d", r=2),
                           kv32[0:n, 256:512].rearrange("p (g d) -> p g d", d=64).unsqueeze(2).to_broadcast([n, 4, 2, 64]),
                           ["kv32"], [f"vdup{1 + bi}"])
                        if sample:
                            ST(nks_d[l, s - 1], kv32[0:n, 0:256], ["kv32"])
                            ST(nvs_d[l, s - 1], kv32[0:n, 256:512], ["kv32"])
                        elif is_last_prompt_tile and bi == 3:
                            ST(wkp_d[l], kv32[0:n, 0:256], ["kv32"])
                            ST(wvp_d[l], kv32[0:n, 256:512], ["kv32"])

                        def do_trk(k16_=k16_, k16k=k16k, n=n, c0=c0):
                            def trk(e):
                                for g in range(4):
                                    ins = e.transpose(trp[0:64, g * 128:g * 128 + n], k16_[0:n, g * 64:(g + 1) * 64], ident16[0:n, 0:n])
                                return ins
                            S.op("pe", trk, [k16k, "ident16"], ["trp"])
                            CP("dve", kT[:, :, 128 + c0:128 + c0 + n], trp[0:64, 0:512].rearrange("p (g t) -> p g t", t=128)[:, :, 0:n],
                               ["trp"], ["kT"])
                        pend_tr.append(do_trk)
                ws.done()
            while pend_tr:
                pend_tr.pop(0)()

            if STOP == "tm":
                DBG("qT", big16[0:64, :], [64, KC * 512], ["qT"])
                DBG("kT", kT[:, :, 128:640], [64, 4, 512], ["kT"])
                DBG("vdup", vdup[:, 1:5].rearrange("p b g d -> p (b g d)"), [128, 4 * 4 * 128], [f"vdup{i}" for i in range(5)])
            stop("tm")
            def fm_chunk(wv, wk, i):
                pj, pk = next_pj()

                def mm(e, wv=wv, pj=pj, i=i):
                    for kc in range(KC):
                        ins = e.matmul(pj[:, 0:T], lhsT=wv[:, kc, i * 128:(i + 1) * 128], rhs=hT[:, kc, 0:T],
                                       start=(kc == 0), stop=(kc == KC - 1))
                    return ins
                S.op("pe", mm, [wk, "hT"], [pk])
                return pj, pk

            def fm_pair(hb, pp):
                tails = []
                w, wk = ws.get()
                wv = w[:, :].rearrange("p (k n) -> p k n", n=512)
                for hi in range(2):
                    h = 2 * hb + hi
                    pj, pk = fm_chunk(wv, wk, hi)
                    E, ek = faS[hi][0], f"faE{hi}"
                    L1, l1k = faS[hi][1], f"faL{hi}"
                    L2, l2k = faS[hi][2], f"faM{hi}"
                    ACT(E[:, 0:T], pj[:, 0:T], AF.Exp, [pk], [ek], scale=-1.0)
                    ACT(L1[:, 0:T], E[:, 0:T], AF.Ln, [ek], [l1k], bias=1.0)
                    ACT(E[:, 0:T], L1[:, 0:T], AF.Exp, [l1k], [ek], scale=-1.0)
                    ACT(L2[:, 0:T], E[:, 0:T], AF.Ln, [ek, "lbv", "lb1"], [l2k], scale=lb1[:, h, l:l + 1],
                        bias=lbv[:, h, l:l + 1])
                    S.op("dve", lambda e, L1=L1, L2=L2: e.tensor_tensor_scan(out=L1[:, 0:T], data0=scanm[:, 0:T], data1=L2[:, 0:T],
                                                                          initial=0.0, op0=ALU.mult, op1=ALU.add),
                         ["scanm", l2k], [l1k])
                    TS("dve", E[:, 0:T], E[:, 0:T], nlb1[:, h, l:l + 1], lb1[:, h, l:l + 1], ALU.mult, ALU.add,
                       [ek, "nlb1", "lb1"], [ek])

                    def tail(hi=hi, E=E, ek=ek, L1=L1, l1k=l1k, L2=L2, l2k=l2k):
                        enb, enk = next_scr()
                        ACT(L2[:, 0:T], L1[:, 0:T], AF.Exp, [l1k], [l2k])
                        ACT(enb[:, 0:T], L1[:, 0:T], AF.Exp, [l1k], [enk], scale=-1.0)
                        TT("dve", Kt16[hi][:, 0:T], E[:, 0:T], enb[:, 0:T], ALU.mult, [ek, enk], [f"Kt16_{hi}"])
                        ebv = L2[:, 0:T].rearrange("p (c s) -> p c s", s=32)
                        TT("dve", kh16[hi][:, 0:T].rearrange("p (c s) -> p c s", s=32),
                           Kt16[hi][:, 0:T].rearrange("p (c s) -> p c s", s=32),
                           ebv[:, :, 31:32].to_broadcast([128, nch, 32]), ALU.mult, [f"Kt16_{hi}", l2k], [f"kh16_{hi}"])
                        CP("dve", eend[pp][hi][:, 0:nch], ebv[:, :, 31], [l2k], [f"eend{pp}{hi}"])
                    tails.append(tail)
                    if hi == 1:
                        tails.pop(0)()
                    yield hi
                for hi in range(2):
                    pj, pk = fm_chunk(wv, wk, 2 + hi)
                    ACT(Va16[hi][:, 0:T], pj[:, 0:T], AF.Copy, [pk], [f"Va16_{hi}"])
                    if hi == 0:
                        tails.pop(0)()
                    yield 2 + hi
                ws.done()
                w, wk = ws.get()
                wv = w[:, :].rearrange("p (k n) -> p k n", n=512)
                for hi in range(2):
                    pj, pk = fm_chunk(wv, wk, hi)
                    q, qk = next_scr()
                    ACT(q[:, 0:T], pj[:, 0:T], AF.Silu, [pk], [qk])
                    TT("dve", Qt16[pp][hi][:, 0:T], q[:, 0:T], faS[hi][2][:, 0:T], ALU.mult, [qk, f"faM{hi}"], [f"Qt16_{pp}{hi}"])
                    yield 4 + hi
                for hi in range(2):
                    pj, pk = fm_chunk(wv, wk, 2 + hi)
                    ACT(sgt16[hi][:, 0:T], pj[:, 0:T], AF.Silu, [pk], [f"sgt16_{hi}"])
                    yield 6 + hi
                ws.done()
                w, wk = ws.get()
                wv = w[:, :].rearrange("p (k n) -> p k n", n=512)
                for jj in range(2):
                    j = 2 * hb + jj
                    pj, pk = fm_chunk(wv, wk, jj)
                    ACT(ub16[:, j, 0:T], pj[:, 0:T], AF.Silu, [pk], [f"ub{j}"])
                    yield 8 + jj
                for hi in range(2):
                    pj, pk = fm_chunk(wv, wk, 2 + hi)
                    sg_, sgk = next_scr()
                    ACT(sg_[:, 0:T], pj[:, 0:T], AF.Sigmoid, [pk], [sgk])
                    TT("dve", gatea[pp][hi][:, 0:T], sg_[:, 0:T], sgt16[hi][:, 0:T], ALU.mult, [sgk, f"sgt16_{hi}"], [f"gatea{pp}{hi}"])
                    yield 10 + hi
                ws.done()

            def prelude_gen(hb, pp):
                atp16 = atp[:, :].bitcast(BF16)
                rr = 0
                for hi in range(2):
                    for src, dst, dk_ in ((kh16, khtok, "khtok"), (Va16, vtok, "vtok")):
                        for r in range((nch + 7) // 8):
                            ncr = min(8, nch - r * 8)
                            bank, bkey = (trp, "trp") if rr % 2 == 0 else (atp16, "atp")
                            rr += 1

                            def trc(e, src=src, hi=hi, r=r, ncr=ncr, bank=bank):
                                for cc in range(ncr):
                                    c = r * 8 + cc
                                    ins = e.transpose(bank[0:32, cc * 128:(cc + 1) * 128], src[hi][:, c * 32:(c + 1) * 32], ident16[:])
                                return ins
                            S.op("pe", trc, [f"{'kh16' if src is kh16 else 'Va16'}_{hi}", "ident16"], [bkey])
                            CP("act" if rr % 2 == 0 else "dve", dst[hi][:, r * 8:r * 8 + ncr, :],
                               bank[0:32, 0:ncr * 128].rearrange("p (c d) -> p c d", d=128), [bkey], [f"{dk_}{hi}"])
                            yield

                    def mat(e, hi=hi):
                        for c in range(nch):
                            ins = e.matmul(atp[0:32, c * 32:(c + 1) * 32], lhsT=Kt16[hi][:, c * 32:(c + 1) * 32],
                                           rhs=Qt16[pp][hi][:, c * 32:(c + 1) * 32], start=True, stop=True)
                        return ins
                    S.op("pe", mat, [f"Kt16_{hi}", f"Qt16_{pp}{hi}"], ["atp"])
                    TT("dve", AT16[hi][:, 0:T].rearrange("p (c s) -> p c s", s=32), atp[0:32, 0:T].rearrange("p (c s) -> p c s", s=32),
                       mask32[:, :].unsqueeze(1).to_broadcast([32, nch, 32]), ALU.mult, ["atp", "mask32"], [f"AT16_{hi}"])
                    yield

            def chunkloop_gen(hb, pp):
                par = [0, 0]

                def stageA(bt, hi):
                    h = 2 * hb + hi

                    def mds(e):
                        for k in range(4):
                            c = bt * 4 + k
                            ins = e.matmul(dsp[:, k * 128:(k + 1) * 128], lhsT=khtok[hi][:, c, :], rhs=vtok[hi][:, c, :],
                                           start=True, stop=True)
                        return ins
                    S.op("pe", mds, [f"khtok{hi}", f"vtok{hi}"], ["dsp"])
                    for k in range(4):
                        c = bt * 4 + k
                        seg_s = 0 if not sample else (1 + c // 2)
                        seg_start = (c == 0) or (sample and c == 2)
                        seg_end = (c == nch - 1) or (sample and c == 1)
                        p = par[hi]
                        cur, ck = S32x[hi][p], f"S32_{hi}_{p}"
                        nxt, nk_ = S32x[hi][1 - p], f"S32_{hi}_{1 - p}"
                        if seg_start:
                            if seg_s == 0:
                                if is_first_tile:
                                    MS("dve", cur[:], 0.0, [ck])
                                else:
                                    LD(cur[:], stp_d[l, h], [ck], chan="io")
                            else:
                                LD(cur[:], sth_d[l, seg_s - 1, h], [ck], chan="io")
                        CP("dve", S16q[hi][:, k, :], cur[:], [ck], [f"S16_{hi}"])
                        STT(nxt[:], cur[:], eend[pp][hi][:, c:c + 1], dsp[:, k * 128:(k + 1) * 128], ALU.mult, ALU.add,
                            [ck, f"eend{pp}{hi}", "dsp"], [nk_])
                        par[hi] = 1 - p
                        if seg_end:
                            dst = stp_d[l, h] if seg_s == 0 else sts_d[l, seg_s - 1, h]
                            ins = S.op("sp", lambda e, dst=dst, nxt=nxt: e.dma_start(out=dst, in_=nxt[:]), [nk_], (), dma="io")
                            if seg_s != 0 or is_last_prompt_tile:
                                out_dmas.append(ins)

                def stageB(bt, hi):
                    oslot = otp[:, hi * 128:(hi + 1) * 128]

                    def mo4(e):
                        for k in range(4):
                            c = bt * 4 + k
                            e.matmul(oslot[:, k * 32:(k + 1) * 32], lhsT=vtok[hi][:, c, :], rhs=AT16[hi][:, c * 32:(c + 1) * 32],
                                     start=True, stop=False)
                            ins = e.matmul(oslot[:, k * 32:(k + 1) * 32], lhsT=S16q[hi][:, k, :],
                                           rhs=Qt16[pp][hi][:, c * 32:(c + 1) * 32], start=False, stop=True)
                        return ins
                    S.op("pe", mo4, [f"vtok{hi}", f"AT16_{hi}", f"S16_{hi}", f"Qt16_{pp}{hi}"], [f"otp{hi}"])
                    CP("act", o32[hi][:, bt * 128:(bt + 1) * 128], oslot[:, 0:128], [f"otp{hi}"], [f"o32_{hi}"])

                prev = None
                for bt in range(nch // 4):
                    for hi in range(2):
                        if prev is not None:
                            stageB(*prev)
                        stageA(bt, hi)
                        prev = (bt, hi)
                        yield
                stageB(*prev)
                yield
                for hi in range(2):
                    h = 2 * hb + hi
                    ACT(sq16[hi][:, 0:T], o32[hi][:, 0:T], AF.Square, [f"o32_{hi}"], [f"sq16_{hi}"])
                    S.op("pe", lambda e, hi=hi: e.matmul(dnp[:, 0:T], lhsT=ones16[:], rhs=sq16[hi][:, 0:T], start=True, stop=True),
                         [f"sq16_{hi}", "ones16"], ["dnp"])
                    rs, rsk = next_scr()
                    ACT(rs[:, 0:T], dnp[:, 0:T], AF.Ln, ["dnp"], [rsk], scale=1.0 / 128, bias=EPS)
                    ACT(rs[:, 0:T], rs[:, 0:T], AF.Exp, [rsk], [rsk], scale=-0.5)
                    STT(rs[:, 0:T], o32[hi][:, 0:T], hg[:, l, h:h + 1], rs[:, 0:T], ALU.mult, ALU.mult,
                        [f"o32_{hi}", "hg", rsk], [rsk])
                    TT("dve", ua16[:, h, 0:T], rs[:, 0:T], gatea[pp][hi][:, 0:T], ALU.mult, [rsk, f"gatea{pp}{hi}"], [f"ua{h}"])
                    yield

            def attn_unit_g(g, c0, nq, kblocks, zero):
                for bi, (kc0, nk, vb, vblk) in enumerate(kblocks):
                    sps, spk = (atp, "atp") if bi == 0 else next_pj()

                    def ms(e, sps=sps, kc0=kc0, nk=nk):
                        for hl in range(4):
                            ins = e.matmul(sps[0:nk, hl * nq:(hl + 1) * nq], lhsT=kT[:, g, kc0:kc0 + nk],
                                           rhs=qT[:, 4 * g + hl, c0:c0 + nq], start=True, stop=True)
                        return ins
                    S.op("pe", ms, ["kT", "qT"], [spk])
                    ACT(PT[bi][0:nk, 0:4 * nq], sps[0:nk, 0:4 * nq], AF.Exp, [spk], [f"PT{bi}"], scale=0.125)
                    if zero:
                        pv = PT[bi][:, 0:512].rearrange("p (h q) -> p h q", q=128)
                        if vb == "prev":
                            MS("dve", pv[0:64, :, 64:128], 0.0, [f"PT{bi}"])
                        else:
                            MS("dve", pv[64:128, :, 0:64], 0.0, [f"PT{bi}"])
                yield

                def mpv(e):
                    for bi, (kc0, nk, vb, vblk) in enumerate(kblocks):
                        ins = e.matmul(dsp[:, 0:4 * nq], lhsT=vdup[0:nk, vblk, g, :], rhs=PT[bi][0:nk, 0:4 * nq],
                                       start=(bi == 0), stop=(bi == len(kblocks) - 1))
                    for bi, (kc0, nk, vb, vblk) in enumerate(kblocks):
                        ins = e.matmul(dnp[:, 0:4 * nq], lhsT=ones16[0:nk, :], rhs=PT[bi][0:nk, 0:4 * nq],
                                       start=(bi == 0), stop=(bi == len(kblocks) - 1))
                    return ins
                S.op("pe", mpv, [f"PT{bi}" for bi in range(len(kblocks))] + [f"vdup{kb[3]}" for kb in kblocks] + ["ones16"],
                     ["dsp", "dnp"])
                den, dk2 = next_scr()
                TT("dve", den[:, 0:4 * nq].rearrange("p (h q) -> p h q", q=nq),
                   dnp[:, 0:4 * nq].rearrange("p (h q) -> p h q", q=nq),
                   esink[:, l * 16 + 4 * g:l * 16 + 4 * g + 4].unsqueeze(2).to_broadcast([128, 4, nq]), ALU.add,
                   ["dnp", "esink"], [dk2])
                ACT(den[:, 0:4 * nq], den[:, 0:4 * nq], AF.Ln, [dk2], [dk2])
                ACT(den[:, 0:4 * nq], den[:, 0:4 * nq], AF.Exp, [dk2], [dk2], scale=-1.0)
                tmp, tk = next_scr()
                TT("dve", tmp[:, 0:4 * nq], dsp[:, 0:4 * nq], den[:, 0:4 * nq], ALU.mult, ["dsp", dk2], [tk])
                for half in range(2):
                    r0 = half * 64
                    tv = tmp[r0:r0 + 64, 0:4 * nq].rearrange("p (j x q) -> p j x q", x=2, q=nq)[:, :, half, :]
                    uv = ub16[r0:r0 + 64, 2 * g:2 * g + 2, c0:c0 + nq]
                    TT("dve", uv, tv, uv, ALU.mult, [tk, f"ub{2 * g}", f"ub{2 * g + 1}"], [f"ub{2 * g}", f"ub{2 * g + 1}"])
                yield

            def attn_gen(g):
                for qb in range(4):
                    kbl = []
                    if not (is_first_tile and qb == 0):
                        kbl.append((qb * 128, 128, "prev", qb))
                    kbl.append((128 + qb * 128, 128, "own", qb + 1))
                    yield from attn_unit_g(g, qb * 128, 128, kbl, True)

            def step(gen, n=1):
                if gen is None:
                    return None
                for _ in range(n):
                    try:
                        next(gen)
                    except StopIteration:
                        return None
                return gen

            def drain(gen):
                while gen is not None:
                    gen = step(gen)

            if not sample:
                core = None
                attn = []

                def step_attn(n):
                    for _ in range(n):
                        if attn:
                            attn[0] = step(attn[0], 1)
                            if attn[0] is None:
                                attn.pop(0)

                for hb in range(4):
                    pp = hb % 2
                    prel = None
                    for ci in fm_pair(hb, pp):
                        if ci < 9:
                            core = step(core, 1)
                            step_attn(1)
                        else:
                            if prel is None:
                                drain(core)
                                core = None
                                prel = prelude_gen(hb, pp)
                            prel = step(prel, 4)
                    drain(prel)
                    core = chunkloop_gen(hb, pp)
                    attn.append(attn_gen(hb))
                while core is not None or attn:
                    core = step(core, 1)
                    if attn:
                        attn[0] = step(attn[0], 1)
                        if attn[0] is None:
                            attn.pop(0)
                CP("dve", khist[:, l, :, :], kT[:, :, 512:640], ["kT"], ["khist"])
                CP("dve", vhist[:, l, :, :], vdup[:, 4, :, :], ["vdup4"], ["vhist"])
            else:
                for hb in range(4):
                    pp = hb % 2
                    for _ in fm_pair(hb, pp):
                        pass
                    drain(prelude_gen(hb, pp))
                    drain(chunkloop_gen(hb, pp))
                for si in range(2):
                    LD(kv32[:, 0:256], cwk_d[l, si], ["kv32"], chan="io")
                    LD(kv32[:, 256:512], cwv_d[l, si], ["kv32"], chan="io")
                    CP("dve", k16[0][:, :], kv32[:, 0:256], ["kv32"], ["k16_0"])
                    CP("dve", vdup[:, 0, :, :].rearrange("p g (r d) -> p g r d", r=2),
                       kv32[:, 256:512].rearrange("p (g d) -> p g d", d=64).unsqueeze(2).to_broadcast([128, 4, 2, 64]),
                       ["kv32"], ["vdup0"])

                    def trk2(e):
                        for g in range(4):
                            ins = e.transpose(trp[0:64, g * 128:(g + 1) * 128], k16[0][:, g * 64:(g + 1) * 64], ident16[:])
                        return ins
                    S.op("pe", trk2, ["k16_0", "ident16"], ["trp"])
                    CP("dve", kT[:, :, 0:128], trp[0:64, 0:512].rearrange("p (g t) -> p g t", t=128), ["trp"], ["kT"])
                    for g in range(4):
                        drain(attn_unit_g(g, si * 64, 64, [(0, 128, "hist", 0), (128 + si * 64, 64, "own", 1 + si)], False))
            if STOP == "attn":
                DBG("ub", ub16[:].rearrange("p h t -> p (h t)"), [128, 8 * 512], [f"ub{h}" for h in range(8)])
            stop("attn")
            first_merge = True
            for j in range(16):
                w, wk = ws.get()
                wma = w[:, 0:2048].rearrange("p (k n) -> p k n", n=128)
                wmb = w[:, 2048:4096].rearrange("p (k n) -> p k n", n=128)
                wpa = w[:, 4096:5120].rearrange("p (k n) -> p k n", n=128)
                wpb = w[:, 5120:6144].rearrange("p (k n) -> p k n", n=128)
                t1 = None
                for which, wg, wp, u, ukeys in (("a", wma, wpa, ua16, [f"ua{h}" for h in range(8)]),
                                                ("b", wmb, wpb, ub16, [f"ub{h}" for h in range(8)])):
                    pj, pk = next_pj()

                    def mg(e, wg=wg, pj=pj):
                        for kc in range(KC):
                            ins = e.matmul(pj[:, 0:T], lhsT=wg[:, kc, :], rhs=hT[:, kc, 0:T], start=(kc == 0), stop=(kc == KC - 1))
                        return ins
                    S.op("pe", mg, [wk, "hT"], [pk])
                    sg_, sgk = sgt16[0 if which == "a" else 1], f"sgt16_{0 if which == 'a' else 1}"
                    ACT(sg_[:, 0:T], pj[:, 0:T], AF.Sigmoid, [pk], [sgk])
                    pj2, pk2 = next_pj()

                    def mp(e, wp=wp, pj2=pj2, u=u):
                        for kc in range(8):
                            ins = e.matmul(pj2[:, 0:T], lhsT=wp[:, kc, :], rhs=u[:, kc, 0:T], start=(kc == 0), stop=(kc == 7))
                        return ins
                    S.op("pe", mp, [wk] + ukeys, [pk2])
                    if which == "a":
                        t1, t1k = next_scr()
                        TT("dve", t1[:, 0:T], pj2[:, 0:T], sg_[:, 0:T], ALU.mult, [pk2, sgk], [t1k])
                    else:
                        t2, t2k = next_scr()
                        TT("dve", t2[:, 0:T], pj2[:, 0:T], sg_[:, 0:T], ALU.mult, [pk2, sgk], [t2k])
                        wr = [f"mg{j}"] + (["big16", "qT"] if first_merge else [])
                        rd_ = [t1k, t2k] + ([] if first_merge else ["big16"])
                        TT("dve", mg16[:, j, 0:T], t1[:, 0:T], t2[:, 0:T], ALU.add, rd_, wr)
                        first_merge = False
                ws.done()

            if STOP == "merge":
                DBG("mg", big16[:], [128, KC * 512], [f"mg{j}" for j in range(16)])
            stop("merge")
            pend_sq = []
            for gq in range(4):
                w, wk = ws.get()
                wv = w[:, :].rearrange("p (k n) -> p k n", n=512)
                for i in range(4):
                    j = gq * 4 + i
                    pj, pk = next_pj()

                    def mo2(e, wv=wv, pj=pj, i=i):
                        for kc in range(KC):
                            ins = e.matmul(pj[:, 0:T], lhsT=wv[:, kc, i * 128:(i + 1) * 128], rhs=mg16[:, kc, 0:T],
                                           start=(kc == 0), stop=(kc == KC - 1))
                        return ins
                    S.op("pe", mo2, [wk] + [f"mg{jj}" for jj in range(16)] + ["big16"], [pk])
                    for (c0, n, s) in segs:
                        STT(xT[:, j, c0:c0 + n], pj[:, c0:c0 + n], modT[:, l, 32 + j, s:s + 1], xT[:, j, c0:c0 + n],
                            ALU.mult, ALU.add, [pk, "modT", "xT"], ["xT"])
                    if pend_sq:
                        pend_sq.pop(0)()
                    sq, sk = sq16[j % 2], f"sq16_{j % 2}"
                    ACT(sq[:, 0:T], xT[:, j, 0:T], AF.Square, ["xT"], [sk])
                    pend_sq.append(lambda sq=sq, sk=sk, j=j: S.op(
                        "pe", lambda e: e.matmul(dnp[:, 0:T], lhsT=ones16[:], rhs=sq[:, 0:T], start=(j == 0), stop=(j == 15)),
                        [sk, "ones16"], ["dnp"]))
                ws.done()
            while pend_sq:
                pend_sq.pop(0)()
            S.op("dve", lambda e: e.memset(qT[0:1, 0, 0:1], 0.0), (), ["big16", "qT"] + [f"mg{jj}" for jj in range(16)])

        tok0 = 0
        for (ti, T) in tiles:
            if STOP == "mod":
                break
            LD(xT[:, :, 0:T], xT_d[:, :, tok0:tok0 + T].rearrange("k p t -> p k t"), ["xT"])
            if T == 512:
                LD(rope[:, 0:4, :], rope_d[:, ti * 4:(ti + 1) * 4, :], ["rope"])
            else:
                LD(rope[:, 0:1, :], rope_d[:, NTB - 1:NTB, :], ["rope"])
            try:
                for l in range(DEPTH):
                    layer_tile(ti, T, l, is_first_tile=(ti == 0), is_last_prompt_tile=(ti == NPT - 1))
            except _Stop:
                break
            rstd, rk = rms_rstd(lambda kc: xT[:, kc, 0:T], KC, T, ["xT"], 1.0 / D, pre=True)
            for kc in range(KC):
                y, yk = next_scr()
                STT(y[:, 0:T], xT[:, kc, 0:T], fng[:, kc:kc + 1], rstd[:, 0:T], ALU.mult, ALU.mult, ["xT", "fng", rk], [yk])
                ST(yT_d[kc, :, tok0:tok0 + T], y[:, 0:T], [yk])
            tok0 += T

        S.wait_all("sp", out_dmas)
        S.finalize()
        with nc.Block() as block:
            @block.tensor
            def _(e):
                S.replay("pe", e, sems)

            @block.scalar
            def _(e):
                S.replay("act", e, sems)

            @block.vector
            def _(e):
                S.replay("dve", e, sems)

            @block.gpsimd
            def _(e):
                S.replay("pool", e, sems)

            @block.sync
            def _(e):
                S.replay("sp", e, sems)
    return nc


def _grp(wcols):
    K, n = wcols.shape
    return np.ascontiguousarray(wcols.reshape(K // 128, 128, n).transpose(1, 0, 2)).reshape(128, -1)


def _pad(a):
    out = np.zeros((128, GW), np.float32)
    out[:, :a.shape[1]] = a
    return out


def _weight_stream(w_in, w_pa, w_pb, w_out, DEPTH):
    groups = []
    for l in range(DEPTH):
        wi = w_in[l]
        c = lambda name, i, n=128: wi[:, OFF[name] + i * n: OFF[name] + (i + 1) * n]
        groups.append(_grp(wi[:, OFF["qb"]:OFF["qb"] + 512]))
        groups.append(_grp(wi[:, OFF["qb"] + 512:OFF["qb"] + 1024]))
        groups.append(_grp(wi[:, OFF["kb"]:OFF["kb"] + 512]))
        for hb in range(4):
            h0, h1 = 2 * hb, 2 * hb + 1
            groups.append(_grp(np.concatenate([c("fa", h0), c("fa", h1), c("ia", h0), c("ia", h1)], axis=1)))
            groups.append(_grp(np.concatenate([c("qa", h0), c("qa", h1), c("za", h0), c("za", h1)], axis=1)))
            groups.append(_grp(np.concatenate([c("zb", h0), c("zb", h1), c("ga", h0), c("ga", h1)], axis=1)))
        for j in range(16):
            a = np.concatenate([_grp(c("ma", j)), _grp(c("mb", j)),
                                _grp(w_pa[l][:, j * 128:(j + 1) * 128]), _grp(w_pb[l][:, j * 128:(j + 1) * 128])], axis=1)
            groups.append(_pad(a))
        for gq in range(4):
            groups.append(_grp(w_out[l][:, gq * 512:(gq + 1) * 512]))
    return np.stack(groups)


def _rope_table(SEQ):
    NTB = SEQ // 128 + 1
    inv = (500000.0 ** (-np.arange(0, 16, 2) / 16)).astype(np.float32)
    tab = np.zeros((128, NTB, 32), np.float32)
    for tb in range(NTB):
        if tb < NTB - 1:
            pos = (tb * 128 + np.arange(128)).astype(np.float32)
        else:
            pos = (PAST + (np.arange(128) % 64)).astype(np.float32)
        ang = pos[:, None] * inv[None, :]
        cs, sn = np.cos(ang).astype(np.float32), np.sin(ang).astype(np.float32)
        tab[:, tb, 0:8] = cs
        tab[:, tb, 8:16] = cs
        tab[:, tb, 16:24] = -sn
        tab[:, tb, 24:32] = sn
    return tab


_NC_CACHE = {}


def kernel(x_prompt, x_sample, c_prompt, c_sample, state_hgrn, cache_win_k, cache_win_v,
           ada_w, ada_b, norm_g, w_in, lb_logits, hgrn_norm_g, sinks, w_branch_a, w_branch_b, w_out,
           final_norm_g):
    f = lambda a: np.asarray(a, dtype=np.float32)
    x_prompt, x_sample, c_prompt, c_sample = f(x_prompt), f(x_sample), f(c_prompt), f(c_sample)
    state_hgrn, cache_win_k, cache_win_v = f(state_hgrn), f(cache_win_k), f(cache_win_v)
    ada_w, ada_b, norm_g, w_in, lb_logits = f(ada_w), f(ada_b), f(norm_g), f(w_in), f(lb_logits)
    hgrn_norm_g, sinks, w_pa, w_pb, w_out, final_norm_g = f(hgrn_norm_g), f(sinks), f(w_branch_a), f(w_branch_b), f(w_out), f(final_norm_g)
    NCORES = x_prompt.shape[0]
    SEQ = x_prompt.shape[1]
    DEPTH = w_in.shape[0]
    NBUF = CFG["NBUF"]
    key = (SEQ, DEPTH, NBUF, STOP, SKIP)
    if key not in _NC_CACHE:
        _NC_CACHE[key] = build_nc(SEQ, DEPTH, NBUF)
    nc = _NC_CACHE[key]

    adaw = np.stack([_grp(ada_w[l][:, g * 512:(g + 1) * 512]) for l in range(DEPTH) for g in range(12)])
    wst = _weight_stream(w_in, w_pa, w_pb, w_out, DEPTH)
    adab = np.ascontiguousarray(ada_b.reshape(DEPTH, 48, 128).transpose(2, 0, 1))
    normg = np.ascontiguousarray(norm_g.reshape(DEPTH, KC, 128).transpose(2, 0, 1))
    lbl4 = np.zeros((4, 1024), np.float32) - 1e4
    lbl4[:DEPTH] = lb_logits
    lbl = np.ascontiguousarray(lbl4.reshape(4, 8, 128).transpose(2, 1, 0))
    hgl = np.ascontiguousarray(hgrn_norm_g.reshape(DEPTH, 8, 128).transpose(2, 0, 1))
    fng = np.ascontiguousarray(final_norm_g.reshape(KC, 128).T)
    rope = _rope_table(SEQ)
    ident = np.eye(128, dtype=np.float32).astype(ml_dtypes.bfloat16)
    m32 = (np.arange(32)[:, None] <= np.arange(32)[None, :]).astype(np.float32)
    mask32 = m32
    scanm = np.ones((128, 512), np.float32)
    scanm[:, ::32] = 0.0
    mrow = np.zeros((1, 4, 512), np.float32)
    mrow[0, 0, 0:64] = 1.0
    mrow[0, 1, 64:128] = 1.0
    qv = mrow[0, 2].reshape(4, 128)
    qv[:, 64:128] = -1000.0
    qv = mrow[0, 3].reshape(4, 128)
    qv[:, 0:64] = -1000.0
    shared = dict(adaw=adaw, wst=wst, adab=adab, normg=normg, lbl=lbl, hg=hgl, sinks=np.ascontiguousarray(sinks.reshape(-1)),
                  fng=fng, rope=rope, ident=ident, mask32=mask32.astype(ml_dtypes.bfloat16),
                  scanm=scanm.astype(ml_dtypes.bfloat16))
    in_maps = []
    for b in range(NCORES):
        xs = np.concatenate([x_prompt[b], x_sample[2 * b], x_sample[2 * b + 1]], axis=0)
        xT = np.ascontiguousarray(xs.T.reshape(KC, 128, -1))
        cs = np.stack([c_prompt[b], c_sample[2 * b], c_sample[2 * b + 1]], axis=1)
        cT = np.ascontiguousarray(cs.reshape(KC, 128, 3).transpose(1, 0, 2))
        m = dict(shared)
        m.update(xT=xT, cT=cT,
                 sth=np.ascontiguousarray(state_hgrn[:, 2 * b:2 * b + 2]),
                 cwk=np.ascontiguousarray(cache_win_k[:, 2 * b:2 * b + 2].reshape(DEPTH, 2, 128, 256)),
                 cwv=np.ascontiguousarray(cache_win_v[:, 2 * b:2 * b + 2].reshape(DEPTH, 2, 128, 256)))
        in_maps.append(m)
    res = run_bass_kernel_spmd(nc, in_maps, core_ids=list(range(NCORES)))
    R = res.results
    LAST["R"] = R
    y_prompt = np.stack([R[b]["yT"][:, :, :SEQ].reshape(D, SEQ).T for b in range(NCORES)])
    y_sample = np.stack([R[b]["yT"][:, :, SEQ + 64 * s:SEQ + 64 * (s + 1)].reshape(D, 64).T
                         for b in range(NCORES) for s in range(2)])
    stp = np.stack([R[b]["stp"] for b in range(NCORES)], axis=1)
    wkp = np.stack([R[b]["wkp"].reshape(DEPTH, 128, 4, 64) for b in range(NCORES)], axis=1)
    wvp = np.stack([R[b]["wvp"].reshape(DEPTH, 128, 4, 64) for b in range(NCORES)], axis=1)
    sts = np.concatenate([R[b]["sts"] for b in range(NCORES)], axis=1)
    nks = np.concatenate([R[b]["nks"].reshape(DEPTH, 2, 64, 4, 64) for b in range(NCORES)], axis=1)
    nvs = np.concatenate([R[b]["nvs"].reshape(DEPTH, 2, 64, 4, 64) for b in range(NCORES)], axis=1)
    o = lambda a: np.ascontiguousarray(a, dtype=np.float32)
    return (o(y_prompt), o(y_sample), o(stp), o(wkp), o(wvp), o(sts), o(nks), o(nvs))
```
